# Optimizing a Trainium2 kernel written in Bass

```python
import jax, jax.numpy as jnp
from jax import lax
import numpy as np

D_MODEL = 2048
BATCH = 1
SEQ = 16384
DEPTH = 2

CHUNK = 64
Q_BLOCK = 128
N_A_LAYERS = DEPTH // 2
N_B_LAYERS = DEPTH - N_A_LAYERS
EPS = 1e-6

GLA_HEADS = 4
GLA_QK_DIM = D_MODEL // 2
GLA_V_DIM = D_MODEL
GLA_DK = GLA_QK_DIM // GLA_HEADS
GLA_DV = GLA_V_DIM // GLA_HEADS
GLA_GATE_RANK = 16
GLA_GATE_TAU = 16.0
GLA_IN_DIM = 2 * GLA_QK_DIM + 2 * GLA_V_DIM + GLA_GATE_RANK

SB_HEADS = 16
SB_HEAD_DIM = D_MODEL // SB_HEADS
SB_KV_HEADS = 4
SB_GROUP = SB_HEADS // SB_KV_HEADS
SB_KV_DIM = SB_KV_HEADS * SB_HEAD_DIM

D_FF = 4 * D_MODEL

kernel_name = "yoco_gla_stickbreaking_hybrid"


def rms_norm(x, g):
    xf = x.astype(jnp.float32)
    y = xf * lax.rsqrt(jnp.mean(xf * xf, axis=-1, keepdims=True) + EPS)
    return (y * g.astype(jnp.float32)).astype(x.dtype)


def sq_relu_mlp(x, w_up, w_down):
    return jnp.square(jax.nn.relu(x @ w_up)) @ w_down


def _to_chunks(t, n_heads, d):
    b, s, _ = t.shape
    return t.reshape(b, s // CHUNK, CHUNK, n_heads, d).transpose(1, 0, 3, 2, 4)


def gla_mixer(x, w_in, w_gate_up, b_gate, g_head, w_out):
    b, s, _ = x.shape
    proj = x @ w_in
    i0 = GLA_QK_DIM
    i1 = 2 * GLA_QK_DIM
    i2 = i1 + GLA_V_DIM
    i3 = i2 + GLA_V_DIM
    q = proj[..., :i0].astype(jnp.float32) * (GLA_DK ** -0.5)
    k = proj[..., i0:i1].astype(jnp.float32)
    v = proj[..., i1:i2].astype(jnp.float32)
    r = proj[..., i2:i3]
    a_low = proj[..., i3:]
    log_alpha = jax.nn.log_sigmoid((a_low @ w_gate_up + b_gate).astype(jnp.float32)) / GLA_GATE_TAU

    qc = _to_chunks(q, GLA_HEADS, GLA_DK)
    kc = _to_chunks(k, GLA_HEADS, GLA_DK)
    vc = _to_chunks(v, GLA_HEADS, GLA_DV)
    gc = _to_chunks(log_alpha, GLA_HEADS, GLA_DK)
    causal = jnp.tril(jnp.ones((CHUNK, CHUNK), dtype=bool))

    def step(state, inp):
        q_i, k_i, v_i, g_i = inp
        cum = jnp.cumsum(g_i, axis=-2)
        o_inter = jnp.einsum('bhik,bhkv->bhiv', q_i * jnp.exp(cum), state)
        diff = cum[:, :, :, None, :] - cum[:, :, None, :, :]
        decay = jnp.exp(jnp.where(causal[:, :, None], diff, -jnp.inf))
        scores = jnp.einsum('bhik,bhjk,bhijk->bhij', q_i, k_i, decay)
        o_intra = jnp.einsum('bhij,bhjv->bhiv', scores, v_i)
        last = cum[:, :, -1:, :]
        new_state = (jnp.exp(last[:, :, 0, :])[..., None] * state
                     + jnp.einsum('bhjk,bhjv->bhkv', k_i * jnp.exp(last - cum), v_i))
        return new_state, o_inter + o_intra

    s0 = jnp.zeros((b, GLA_HEADS, GLA_DK, GLA_DV), jnp.float32)
    _, o = lax.scan(step, s0, (qc, kc, vc, gc))
    o = o.transpose(1, 0, 3, 2, 4).reshape(b, s, GLA_HEADS, GLA_DV).astype(x.dtype)
    o = rms_norm(o, g_head)
    o = o * jax.nn.silu(r.reshape(b, s, GLA_HEADS, GLA_DV))
    return o.reshape(b, s, GLA_V_DIM) @ w_out


def shared_kv(h, g, w_kv):
    b, s, _ = h.shape
    kv = rms_norm(h, g) @ w_kv
    k = kv[..., :SB_KV_DIM].reshape(b, s, SB_KV_HEADS, SB_HEAD_DIM).transpose(0, 2, 1, 3)
    v = kv[..., SB_KV_DIM:].reshape(b, s, SB_KV_HEADS, SB_HEAD_DIM).transpose(0, 2, 1, 3)
    return k, v


def stick_breaking_mixer(x, w_q, k_sh, v_sh, w_out):
    b, s, _ = x.shape
    n_blk = s // Q_BLOCK
    q = (x @ w_q).reshape(b, s, SB_KV_HEADS, SB_GROUP, SB_HEAD_DIM)
    q = q.reshape(b, n_blk, Q_BLOCK, SB_KV_HEADS, SB_GROUP, SB_HEAD_DIM).transpose(1, 0, 3, 4, 2, 5)
    key_pos = jnp.arange(s)
    scale = SB_HEAD_DIM ** -0.5

    def one_block(args):
        q_blk, blk = args
        q_pos = blk * Q_BLOCK + jnp.arange(Q_BLOCK)
        mask = key_pos[None, :] < q_pos[:, None]
        z = jnp.einsum('bngqd,bnsd->bngqs', q_blk, k_sh).astype(jnp.float32) * scale
        log_one_minus = jnp.where(mask, -jax.nn.softplus(z), 0.0)
        tail = lax.cumsum(log_one_minus, axis=4, reverse=True) - log_one_minus
        w = jnp.where(mask, jnp.exp(jax.nn.log_sigmoid(z) + tail), 0.0)
        return jnp.einsum('bngqs,bnsd->bngqd', w.astype(v_sh.dtype), v_sh)

    o = lax.map(one_block, (q, jnp.arange(n_blk)))
    o = o.transpose(1, 0, 4, 2, 3, 5).reshape(b, s, SB_HEADS * SB_HEAD_DIM)
    return o @ w_out


def setup_inputs(seed: int = 0) -> dict:
    key = jax.random.key(seed)
    ks = jax.random.split(key, 16)

    def nrm(k, shape, fan_in):
        return jax.random.normal(k, shape, jnp.float32) * (fan_in ** -0.5)

    def gain(k, shape):
        return 1.0 + 0.01 * jax.random.normal(k, shape, jnp.float32)

    return {
        "x": jax.random.normal(ks[0], (BATCH, SEQ, D_MODEL), jnp.float32),
        "norm1_g": gain(ks[1], (DEPTH, D_MODEL)),
        "norm2_g": gain(ks[2], (DEPTH, D_MODEL)),
        "gla_w_in": nrm(ks[3], (N_A_LAYERS, D_MODEL, GLA_IN_DIM), D_MODEL),
        "gla_w_gate_up": nrm(ks[4], (N_A_LAYERS, GLA_GATE_RANK, GLA_QK_DIM), GLA_GATE_RANK),
        "gla_b_gate": 0.1 * jax.random.normal(ks[5], (N_A_LAYERS, GLA_QK_DIM), jnp.float32),
        "gla_head_g": gain(ks[6], (N_A_LAYERS, GLA_DV)),
        "gla_w_out": nrm(ks[7], (N_A_LAYERS, GLA_V_DIM, D_MODEL), GLA_V_DIM),
        "kv_norm_g": gain(ks[8], (D_MODEL,)),
        "kv_w": nrm(ks[9], (D_MODEL, 2 * SB_KV_DIM), D_MODEL),
        "sb_w_q": nrm(ks[10], (N_B_LAYERS, D_MODEL, SB_HEADS * SB_HEAD_DIM), D_MODEL),
        "sb_w_out": nrm(ks[11], (N_B_LAYERS, SB_HEADS * SB_HEAD_DIM, D_MODEL), SB_HEADS * SB_HEAD_DIM),
        "mlp_w_up": nrm(ks[12], (DEPTH, D_MODEL, D_FF), D_MODEL),
        "mlp_w_down": nrm(ks[13], (DEPTH, D_FF, D_MODEL), D_FF),
        "final_g": gain(ks[14], (D_MODEL,)),
    }


def reference(x, norm1_g, norm2_g, gla_w_in, gla_w_gate_up, gla_b_gate, gla_head_g,
              gla_w_out, kv_norm_g, kv_w, sb_w_q, sb_w_out, mlp_w_up, mlp_w_down, final_g):
    h = x
    k_sh = None
    v_sh = None
    for layer in range(DEPTH):
        if layer < N_A_LAYERS:
            h = h + gla_mixer(rms_norm(h, norm1_g[layer]), gla_w_in[layer], gla_w_gate_up[layer],
                              gla_b_gate[layer], gla_head_g[layer], gla_w_out[layer])
        else:
            if layer == N_A_LAYERS:
                k_sh, v_sh = shared_kv(h, kv_norm_g, kv_w)
            j = layer - N_A_LAYERS
            h = h + stick_breaking_mixer(rms_norm(h, norm1_g[layer]), sb_w_q[j], k_sh, v_sh, sb_w_out[j])
        h = h + sq_relu_mlp(rms_norm(h, norm2_g[layer]), mlp_w_up[layer], mlp_w_down[layer])
    return rms_norm(h, final_g)
```

```python
import numpy as np
from contextlib import ExitStack
import concourse.bass as bass
import concourse.mybir as mybir
from concourse.bass_utils import run_bass_kernel_spmd

F32 = mybir.dt.float32
BF16 = mybir.dt.bfloat16
AF = mybir.ActivationFunctionType
ALU = mybir.AluOpType

NCORES = 8
D = 2048
SEQ = 16384
DFF = 8192
EPS = 1e-6
GLA_H, GLA_DK, GLA_DV = 4, 256, 512
GLA_IN = 6160
SB_H, SB_DH, SB_KVH = 16, 128, 4
ENGS = ("pe", "act", "dve", "pool", "sp")


class Buf:
    __slots__ = ("name", "w", "r")

    def __init__(self, name):
        self.name = name
        self.w = {}
        self.r = {}


class Prog:
    def __init__(self, nc):
        self.nc = nc
        self.q = {e: [] for e in ENGS}
        self.cnt = {}
        self.sems = {}
        self.seen = {e: {} for e in ENGS}
        self._stack = []
        self.final = {}

    def _sem(self, key):
        if key not in self.sems:
            cm = self.nc.semaphore("s%d" % len(self.sems))
            self.sems[key] = cm.__enter__()
            self._stack.append(cm)
            self.cnt[key] = 0
        return self.sems[key]

    def close(self):
        for cm in reversed(self._stack):
            cm.__exit__(None, None, None)

    def _deps(self, eng, reads, writes):
        need = {}
        for b in reads:
            for k, v in b.w.items():
                if need.get(k, 0) < v:
                    need[k] = v
        for b in writes:
            for d in (b.w, b.r):
                for k, v in d.items():
                    if need.get(k, 0) < v:
                        need[k] = v
        out = []
        seen = self.seen[eng]
        for k, v in need.items():
            if k == eng and eng == "pe":
                continue
            if seen.get(k, 0) >= v:
                continue
            seen[k] = v
            out.append((k, v))
        return out

    def _record(self, ev, reads, writes):
        k, v = ev
        for b in reads:
            if b.r.get(k, 0) < v:
                b.r[k] = v
        for b in writes:
            if b.w.get(k, 0) < v:
                b.w[k] = v

    def op(self, eng, fn, reads=(), writes=(), inc=True):
        waits = self._deps(eng, reads, writes)
        sem = self._sem(eng)
        if inc:
            self.cnt[eng] += 1
        ev = (eng, self.cnt[eng] if inc else self.cnt[eng] + 1)
        self._record(ev, reads, writes)
        sems = self.sems

        def run(e):
            for k, v in waits:
                e.wait_ge(sems[k], v)
            ins = fn(e)
            if inc:
                ins.then_inc(sem, 1)
        self.q[eng].append(run)

    def dma(self, eng, fn, reads=(), writes=(), key=None, final=False):
        waits = self._deps(eng, reads, writes)
        if key is None:
            key = ("dma", (writes[0] if writes else reads[0]).name)
        sem = self._sem(key)
        self.cnt[key] += 16
        ev = (key, self.cnt[key])
        self._record(ev, reads, writes)
        if final:
            self.final[key] = self.cnt[key]
        sems = self.sems

        def run(e):
            for k, v in waits:
                e.wait_ge(sems[k], v)
            fn(e).then_inc(sem, 16)
        self.q[eng].append(run)

    def coll(self, fn, reads=(), writes=(), key=None):
        eng = "pool"
        waits = self._deps(eng, reads, writes)
        sem = self._sem(key)
        self.cnt[key] += 1
        ev = (key, self.cnt[key])
        self._record(ev, reads, writes)
        sems = self.sems

        def run(e):
            for k, v in waits:
                e.wait_ge(sems[k], v)
            fn(e).then_inc(sem)
        self.q[eng].append(run)

    def barrier(self):
        snap = dict(self.cnt)
        sems = self.sems
        for eng in ENGS:
            waits = []
            seen = self.seen[eng]
            for k, v in snap.items():
                if v == 0 or k == eng or seen.get(k, 0) >= v:
                    continue
                seen[k] = v
                waits.append((k, v))

            def run(e, waits=waits):
                for k, v in waits:
                    e.wait_ge(sems[k], v)
            self.q[eng].append(run)

    def final_wait(self, eng):
        waits = list(self.final.items())
        sems = self.sems

        def run(e):
            for k, v in waits:
                e.wait_ge(sems[k], v)
        self.q[eng].append(run)

    def emit(self):
        nc = self.nc
        q = self.q
        with nc.Block() as block:
            @block.tensor
            def _(e):
                for f in q["pe"]:
                    f(e)

            @block.scalar
            def _(e):
                for f in q["act"]:
                    f(e)

            @block.vector
            def _(e):
                for f in q["dve"]:
                    f(e)

            @block.gpsimd
            def _(e):
                for f in q["pool"]:
                    f(e)

            @block.sync
            def _(e):
                for f in q["sp"]:
                    f(e)


class KB:
    def __init__(self):
        self.nc = bass.Bass("TRN2", target_bir_lowering=False)
        self.P = Prog(self.nc)
        self.es = ExitStack()
        self.nbuf = 0
        self.banks = []
        self.bi = 0
        self.outs = []
        self.rr = 0

    def dram(self, name, shape, dt, kind="Internal"):
        return self.nc.dram_tensor(name, list(shape), dt, kind=kind).ap()

    def inp(self, name, shape, dt=F32):
        return self.dram(name, shape, dt, "ExternalInput")

    def out(self, name, shape, dt=F32):
        ap = self.dram(name, shape, dt, "ExternalOutput")
        return ap, None

    def sb(self, name, shape, dt):
        return self.es.enter_context(self.nc.sbuf_tensor("sb_" + name, list(shape), dt))

    def buf(self, name="b"):
        self.nbuf += 1
        return Buf("%s_%d" % (name, self.nbuf))

    def sbb(self, name, shape, dt):
        return self.sb(name, shape, dt), self.buf(name)

    def init_psum(self, n=8):
        for i in range(n):
            t = self.es.enter_context(self.nc.psum_tensor("ps%d" % i, [128, 512], F32))
            self.banks.append((t, self.buf("ps%d" % i)))

    def bank(self):
        r = self.banks[self.bi % len(self.banks)]
        self.bi += 1
        return r

    def evac_eng(self):
        self.rr += 1
        return "act" if self.rr % 2 else "dve"

    def finish(self):
        self.P.final_wait("sp")
        self.P.emit()
        self.es.close()
        self.P.close()
        return self.nc


def evac_copy(kb, eng, out_ap, in_ap, reads, writes, scale=None):
    P = kb.P
    if eng == "act":
        if scale is None:
            P.op("act", lambda e: e.activation(out=out_ap, in_=in_ap, func=AF.Copy), reads, writes)
        else:
            P.op("act", lambda e: e.activation(out=out_ap, in_=in_ap, func=AF.Copy, scale=float(scale)), reads, writes)
    else:
        if scale is None:
            P.op(eng, lambda e: e.tensor_copy(out=out_ap, in_=in_ap), reads, writes)
        else:
            P.op(eng, lambda e: e.tensor_scalar(out=out_ap, in0=in_ap, scalar1=float(scale), scalar2=None,
                                                 op0=ALU.mult), reads, writes)


def load_consts(kb, ident_d):
    ident, bident = kb.sbb("ident", [128, 128], F32)
    kb.P.dma("sp", lambda e: e.dma_start(out=ident[:], in_=ident_d[:, :]), writes=[bident])
    kb.ident, kb.bident = ident, bident


def load_bcast(kb, name, vec_ap, n):
    t, b = kb.sbb(name, [128, n], F32)
    kb.P.dma("sp", lambda e: e.dma_start(out=t[:], in_=vec_ap.partition_broadcast(128)), writes=[b])
    return t, b


class Rows:
    def __init__(self, kb):
        self.kb = kb
        self.xin = [kb.sbb("r_xin%d" % i, [128, D], F32) for i in range(2)]
        self.xn = [kb.sbb("r_xn%d" % i, [128, D], F32) for i in range(2)]
        self.st = [kb.sbb("r_st%d" % i, [128, 12], F32) for i in range(2)]
        self.mul = None
        self.i = 0

    def run(self, src, row0, ntiles, gt, gb, src_bufs=None, nheads=1, mul_src=None,
            dstT=None, bdstT=None, col0=0, dst_dram=None, dst_bufs=None, final=False):
        kb = self.kb
        P = kb.P
        W = D // nheads
        if mul_src is not None and self.mul is None:
            self.mul = [kb.sbb("r_mul%d" % i, [128, D], BF16) for i in range(2)]
        for ti in range(ntiles):
            r0 = row0 + ti * 128
            tt = r0 // 128
            i2 = self.i % 2
            self.i += 1
            xt, bx = self.xin[i2]
            xnt, bxn = self.xn[i2]
            s, bs = self.st[i2]
            rd = [src_bufs[tt]] if src_bufs is not None else []
            P.dma("sp", lambda e, xt=xt, r0=r0: e.dma_start(out=xt[:], in_=src[r0:r0 + 128, :]), reads=rd, writes=[bx])
            if mul_src is not None:
                mt, bm = self.mul[i2]
                P.dma("sp", lambda e, mt=mt, r0=r0: e.dma_start(out=mt[:], in_=mul_src[r0:r0 + 128, :]), writes=[bm])
            for h in range(nheads):
                P.op("act", lambda e, xt=xt, xnt=xnt, s=s, h=h: e.activation(
                    out=xnt[:, h * W:(h + 1) * W], in_=xt[:, h * W:(h + 1) * W], func=AF.Square, accum_out=s[:, h:h + 1]),
                    reads=[bx], writes=[bxn, bs])
            P.op("act", lambda e, s=s: e.activation(out=s[:, 4:4 + nheads], in_=s[:, 0:nheads], func=AF.Sqrt, scale=1.0 / W,
                                                    bias=kb.eps[:, 0:1]), reads=[bs, kb.beps], writes=[bs])
            P.op("dve", lambda e, s=s: e.reciprocal(out=s[:, 8:8 + nheads], in_=s[:, 4:4 + nheads]), reads=[bs], writes=[bs])
            for h in range(nheads):
                P.op("dve", lambda e, xt=xt, xnt=xnt, s=s, h=h: e.scalar_tensor_tensor(
                    out=xnt[:, h * W:(h + 1) * W], in0=xt[:, h * W:(h + 1) * W], scalar=s[:, 8 + h:9 + h],
                    in1=gt[:, h * W:(h + 1) * W], op0=ALU.mult, op1=ALU.mult),
                    reads=[bx, bs, gb], writes=[bxn])
            if mul_src is not None:
                P.op("dve", lambda e, xnt=xnt, mt=mt: e.tensor_tensor(out=xnt[:], in0=xnt[:], in1=mt[:], op=ALU.mult),
                     reads=[bxn, bm], writes=[bxn])
            if dstT is not None:
                c = col0 + ti * 128
                for b4 in range(4):
                    ps, bps = kb.bank()
                    for cc in range(4):
                        kc = b4 * 4 + cc
                        P.op("pe", lambda e, ps=ps, xnt=xnt, kc=kc, cc=cc: e.transpose(
                            ps[:, cc * 128:(cc + 1) * 128], xnt[:, kc * 128:(kc + 1) * 128], kb.ident[:]),
                            reads=[bxn, kb.bident], writes=[bps], inc=(cc == 3))
                    evac_copy(kb, kb.evac_eng(), dstT[:, b4 * 4:(b4 + 1) * 4, c:c + 128],
                              ps[:, :].rearrange("p (c t) -> p c t", c=4), [bps], [bdstT[c // 128]])
            if dst_dram is not None:
                wr = [dst_bufs[tt]] if dst_bufs is not None else []
                P.dma("sp", lambda e, xnt=xnt, r0=r0: e.dma_start(out=dst_dram[r0:r0 + 128, :], in_=xnt[:]),
                      reads=[bxn], writes=wr, final=final)


def emit_norm_T(kb, src, T, gt, gb, xnT, bxnT, nm):
    if not hasattr(kb, "rows"):
        kb.rows = Rows(kb)
    kb.rows.run(src, 0, T // 128, gt, gb, dstT=xnT, bdstT=bxnT)


class WStream:
    def __init__(self, kb, nbuf=2):
        self.kb = kb
        self.t = [kb.sbb("wblk%d" % i, [128, 16, 512], BF16) for i in range(nbuf)]
        self.i = 0

    def load(self, w_ap, r0, nrows, c0, ncols):
        kb = self.kb
        t, b = self.t[self.i % len(self.t)]
        self.i += 1
        nk = nrows // 128
        src = w_ap[r0:r0 + nrows, c0:c0 + ncols].rearrange("(kc p) n -> p kc n", p=128)
        kb.P.dma("pool", lambda e: e.dma_start(out=t[:, 0:nk, 0:ncols], in_=src), writes=[b])
        return t, b


def proj_tok(kb, ws, xnT, bxnT, T, w_ap, c0, ncols, evac, nk=16, r0=0):
    P = kb.P
    wt, wb = ws.load(w_ap, r0, nk * 128, c0, ncols)
    for tt in range(T // 128):
        ps, bps = kb.bank()
        for kc in range(nk):
            P.op("pe", lambda e, ps=ps, kc=kc, tt=tt: e.matmul(
                ps[:, 0:ncols], lhsT=xnT[:, kc, tt * 128:(tt + 1) * 128], rhs=wt[:, kc, 0:ncols],
                start=(kc == 0), stop=(kc == nk - 1)),
                reads=[bxnT[tt], wb], writes=[bps], inc=(kc == nk - 1))
        evac(tt, ps, bps)


def proj_feat(kb, ws, xnT, bxnT, T, w_ap, c0, ncols, evac, nk=16, tb=512, tok0=0):
    P = kb.P
    wt, wb = ws.load(w_ap, 0, nk * 128, c0, ncols)
    for t0 in range(0, T, tb):
        nt = min(tb, T - t0)
        rd = [bxnT[(tok0 + t0) // 128 + i] for i in range(nt // 128)]
        for j in range((ncols + 127) // 128):
            m = min(128, ncols - j * 128)
            ps, bps = kb.bank()
            for kc in range(nk):
                P.op("pe", lambda e, ps=ps, kc=kc, j=j, m=m, t0=t0, nt=nt: e.matmul(
                    ps[0:m, 0:nt], lhsT=wt[:, kc, j * 128:j * 128 + m], rhs=xnT[:, kc, tok0 + t0:tok0 + t0 + nt],
                    start=(kc == 0), stop=(kc == nk - 1)),
                    reads=rd + [wb], writes=[bps], inc=(kc == nk - 1))
            evac(j, t0, nt, ps, bps)


class Stager:
    def __init__(self, kb, name, shape, dt, n=3):
        self.kb = kb
        self.t = [kb.sbb(name + str(i), shape, dt) for i in range(n)]
        self.i = 0

    def next(self):
        r = self.t[self.i % len(self.t)]
        self.i += 1
        return r


def build_A(T):
    kb = KB()
    P = kb.P
    x = kb.inp("x", [T, D])
    g1 = kb.inp("g1", [D])
    w_in = kb.inp("w_in", [D, GLA_IN])
    wgu = kb.inp("wgu", [16, 1024])
    bg = kb.inp("bg", [1024])
    ident_d = kb.inp("ident", [128, 128])
    q_o, bq_o = kb.out("q", [T, 1024])
    k_o, bk_o = kb.out("k", [T, 1024])
    g_o, bg_o = kb.out("g", [T, 1024])
    v_o, bv_o = kb.out("v", [T, 2048], BF16)
    sr_o, bsr_o = kb.out("sr", [T, 2048], BF16)
    kb.init_psum()
    load_consts(kb, ident_d)
    kb.eps, kb.beps = kb.sbb("eps", [128, 1], F32)
    P.op("dve", lambda e: e.memset(kb.eps[:], EPS), writes=[kb.beps])
    gt, gb = load_bcast(kb, "g1b", g1, D)
    bgt, bgb = load_bcast(kb, "bgb", bg, 1024)
    wgu_t, wgu_b = kb.sbb("wgu", [16, 1024], F32)
    P.dma("sp", lambda e: e.dma_start(out=wgu_t[:], in_=wgu[:, :]), writes=[wgu_b])
    xnT = kb.sb("xnT", [128, 16, T], BF16)
    bxnT = [kb.buf("xnT") for _ in range(T // 128)]
    emit_norm_T(kb, x, T, gt, gb, xnT, bxnT, "n1")
    ws = WStream(kb)
    stf = Stager(kb, "stf", [128, 512], F32)
    stb = Stager(kb, "stb", [128, 512], BF16)

    def mk_evac(dst, bdst, col, kind, scale=None):
        def evac(tt, ps, bps):
            if kind == "f32":
                s, bs = stf.next()
                evac_copy(kb, kb.evac_eng(), s[:], ps[:, :], [bps], [bs], scale=scale)
            elif kind == "bf16":
                s, bs = stb.next()
                evac_copy(kb, kb.evac_eng(), s[:], ps[:, :], [bps], [bs])
            else:
                s, bs = stb.next()
                P.op("act", lambda e: e.activation(out=s[:], in_=ps[:, :], func=AF.Silu), [bps], [bs])
            P.dma("sp", lambda e: e.dma_start(out=dst[tt * 128:(tt + 1) * 128, col:col + 512], in_=s[:]),
                  reads=[bs], final=True)
        return evac

    for nb in range(2):
        proj_tok(kb, ws, xnT, bxnT, T, w_in, nb * 512, 512, mk_evac(q_o, bq_o, nb * 512, "f32", GLA_DK ** -0.5))
    for nb in range(2):
        proj_tok(kb, ws, xnT, bxnT, T, w_in, 1024 + nb * 512, 512, mk_evac(k_o, bk_o, nb * 512, "f32"))
    for nb in range(4):
        proj_tok(kb, ws, xnT, bxnT, T, w_in, 2048 + nb * 512, 512, mk_evac(v_o, bv_o, nb * 512, "bf16"))
    alT, balT = kb.sbb("alT", [16, T], F32)
    def evac_al(j, t0, nt, ps, bps):
        evac_copy(kb, "dve", alT[:, t0:t0 + nt], ps[0:16, 0:nt], [bps], [balT])
    proj_feat(kb, ws, xnT, bxnT, T, w_in, 6144, 16, evac_al)
    ez = [kb.sbb("ez%d" % i, [128, 512], F32) for i in range(2)]
    for tt in range(T // 128):
        for hb in range(2):
            ps, bps = kb.bank()
            P.op("pe", lambda e, ps=ps, tt=tt, hb=hb: e.matmul(
                ps[:, :], lhsT=alT[:, tt * 128:(tt + 1) * 128], rhs=wgu_t[:, hb * 512:(hb + 1) * 512],
                start=True, stop=True), reads=[balT, wgu_b], writes=[bps])
            z, bz = ez[(tt * 2 + hb) % 2]
            P.op("dve", lambda e, z=z, ps=ps, hb=hb: e.tensor_tensor(
                out=z[:], in0=ps[:, :], in1=bgt[:, hb * 512:(hb + 1) * 512], op=ALU.add),
                reads=[bps, bgb], writes=[bz])
            P.op("act", lambda e, z=z: e.activation(out=z[:], in_=z[:], func=AF.Exp, scale=-1.0), [bz], [bz])
            s, bs = stf.next()
            P.op("act", lambda e, z=z, s=s: e.activation(out=s[:], in_=z[:], func=AF.Ln, bias=1.0), [bz], [bs])
            P.op("dve", lambda e, s=s: e.tensor_scalar(out=s[:], in0=s[:], scalar1=-1.0 / 16.0, scalar2=None,
                                                         op0=ALU.mult), [bs], [bs])
            P.dma("sp", lambda e, s=s, tt=tt, hb=hb: e.dma_start(
                out=g_o[tt * 128:(tt + 1) * 128, hb * 512:(hb + 1) * 512], in_=s[:]), reads=[bs], final=True)
    for nb in range(4):
        proj_tok(kb, ws, xnT, bxnT, T, w_in, 4096 + nb * 512, 512, mk_evac(sr_o, bsr_o, nb * 512, "silu"))
    return kb.finish()


def build_B(N):
    kb = KB()
    P = kb.P
    q = kb.inp("q", [N, 256])
    k = kb.inp("k", [N, 256])
    g = kb.inp("g", [N, 256])
    v = kb.inp("v", [N, 256], BF16)
    tri_d = kb.inp("tri", [64, 64])
    idb_d = kb.inp("identb", [64, 64], BF16)
    ones_d = kb.inp("ones", [64, 2])
    o_o, _ = kb.out("o", [N, 256])
    kb.init_psum()
    tri, btri = kb.sbb("tri", [64, 64], F32)
    idb, bidb = kb.sbb("idb", [64, 64], BF16)
    ones, bones = kb.sbb("ones", [64, 2], F32)
    P.dma("sp", lambda e: e.dma_start(out=tri[:], in_=tri_d[:, :]), writes=[btri])
    P.dma("sp", lambda e: e.dma_start(out=idb[:], in_=idb_d[:, :]), writes=[bidb])
    P.dma("sp", lambda e: e.dma_start(out=ones[:], in_=ones_d[:, :]), writes=[bones])
    S, bS = kb.sbb("S", [128, 2, 256], F32)
    Sb, bSb = kb.sbb("Sb", [128, 2, 256], BF16)
    P.op("dve", lambda e: e.memset(S[:], 0.0), writes=[bS])
    P.op("dve", lambda e: e.memset(Sb[:], 0.0), writes=[bSb])
    G = 8
    qg = [kb.sbb("qg%d" % i, [64, G, 256], F32) for i in range(2)]
    kg = [kb.sbb("kg%d" % i, [64, G, 256], F32) for i in range(2)]
    gg = [kb.sbb("gg%d" % i, [64, G, 256], F32) for i in range(2)]
    vg = [kb.sbb("vg%d" % i, [64, G, 256], BF16) for i in range(2)]
    og = [kb.sbb("og%d" % i, [64, G, 256], F32) for i in range(2)]
    ecum = [kb.sbb("ecum%d" % i, [64, 256], F32) for i in range(2)]
    encum = [kb.sbb("encum%d" % i, [64, 256], F32) for i in range(2)]
    qtb = [kb.sbb("qtb%d" % i, [64, 256], BF16) for i in range(2)]
    ktb = [kb.sbb("ktb%d" % i, [64, 256], BF16) for i in range(2)]
    qkT = [kb.sbb("qkT%d" % i, [128, 4, 64], BF16) for i in range(2)]
    sTb = [kb.sbb("sTb%d" % i, [64, 64], BF16) for i in range(2)]
    lam = [kb.sbb("lam%d" % i, [128, 4], F32) for i in range(2)]
    tmp = [kb.sbb("tmp%d" % i, [128, 512], F32) for i in range(2)]
    ngrp = N // (64 * G)

    def load_group(gi):
        sl = slice(gi * 64 * G, (gi + 1) * 64 * G)
        for (tl, src) in ((qg, q), (kg, k), (gg, g), (vg, v)):
            t, b = tl[gi % 2]
            P.dma("sp", lambda e, t=t, src=src: e.dma_start(
                out=t[:], in_=src[sl, :].rearrange("(c p) d -> p c d", p=64)), writes=[b])

    load_group(0)
    n = 0
    for gi in range(ngrp):
        if gi + 1 < ngrp:
            load_group(gi + 1)
        qt_, bq = qg[gi % 2]
        kt_, bk = kg[gi % 2]
        gt_, bgg = gg[gi % 2]
        vt_, bv = vg[gi % 2]
        ot_, bo = og[gi % 2]
        for c in range(G):
            i2 = n % 2
            n += 1
            psA, bA = kb.bank()
            P.op("pe", lambda e, psA=psA, gt_=gt_, c=c: e.matmul(
                psA[0:64, 0:256], lhsT=tri[:, :], rhs=gt_[:, c, :], start=True, stop=True),
                reads=[btri, bgg], writes=[bA])
            psL, bL = kb.bank()
            for cc in range(2):
                P.op("pe", lambda e, psL=psL, gt_=gt_, c=c, cc=cc: e.matmul(
                    psL[:, 2 * cc:2 * cc + 2], lhsT=gt_[:, c, cc * 128:(cc + 1) * 128], rhs=ones[:, 0:2],
                    start=True, stop=True), reads=[bgg, bones], writes=[bL], inc=(cc == 1))
            lm, blm = lam[i2]
            P.op("act", lambda e, lm=lm, psL=psL: e.activation(out=lm[:, 0:4], in_=psL[:, 0:4], func=AF.Exp),
                 reads=[bL], writes=[blm])
            ec, bec = ecum[i2]
            en, ben = encum[i2]
            P.op("act", lambda e, ec=ec, psA=psA: e.activation(out=ec[:], in_=psA[0:64, 0:256], func=AF.Exp),
                 reads=[bA], writes=[bec])
            P.op("act", lambda e, en=en, psA=psA: e.activation(out=en[:], in_=psA[0:64, 0:256], func=AF.Exp, scale=-1.0),
                 reads=[bA], writes=[ben])
            qb, bqb = qtb[i2]
            kbt, bkb = ktb[i2]
            P.op("dve", lambda e, qb=qb, qt_=qt_, ec=ec, c=c: e.tensor_tensor(
                out=qb[:], in0=qt_[:, c, :], in1=ec[:], op=ALU.mult), reads=[bq, bec], writes=[bqb])
            P.op("dve", lambda e, kbt=kbt, kt_=kt_, en=en, c=c: e.tensor_tensor(
                out=kbt[:], in0=kt_[:, c, :], in1=en[:], op=ALU.mult), reads=[bk, ben], writes=[bkb])
            psT, bT = kb.bank()
            for idx, (src, bsrc) in enumerate(((qb, bqb), (qb, bqb), (kbt, bkb), (kbt, bkb))):
                cc = idx % 2
                P.op("pe", lambda e, psT=psT, src=src, cc=cc, idx=idx: e.matmul(
                    psT[:, idx * 64:(idx + 1) * 64], lhsT=src[:, cc * 128:(cc + 1) * 128], rhs=idb[:, :],
                    start=True, stop=True), reads=[bsrc, bidb], writes=[bT], inc=(idx == 3))
            qk, bqk = qkT[i2]
            evac_copy(kb, "act", qk[:, :, :], psT[:, 0:256].rearrange("p (a t) -> p a t", a=4), [bT], [bqk])
            psS, bSs = kb.bank()
            for cc in range(2):
                P.op("pe", lambda e, psS=psS, qk=qk, cc=cc: e.matmul(
                    psS[0:64, 0:64], lhsT=qk[:, 2 + cc, :], rhs=qk[:, cc, :], start=(cc == 0), stop=(cc == 1)),
                    reads=[bqk], writes=[bSs], inc=(cc == 1))
            st, bst = sTb[i2]
            P.op("dve", lambda e, st=st, psS=psS: e.tensor_tensor(
                out=st[:], in0=psS[0:64, 0:64], in1=tri[:, :], op=ALU.mult), reads=[bSs, btri], writes=[bst])
            psO, bO = kb.bank()
            P.op("pe", lambda e, psO=psO, st=st, vt_=vt_, c=c: e.matmul(
                psO[0:64, 0:256], lhsT=st[:, :], rhs=vt_[:, c, :], start=True, stop=False),
                reads=[bst, bv], writes=[bO], inc=False)
            for cc in range(2):
                P.op("pe", lambda e, psO=psO, qk=qk, cc=cc: e.matmul(
                    psO[0:64, 0:256], lhsT=qk[:, cc, :], rhs=Sb[:, cc, :], start=False, stop=(cc == 1)),
                    reads=[bqk, bSb], writes=[bO], inc=(cc == 1))
            evac_copy(kb, "act", ot_[:, c, :], psO[0:64, 0:256], [bO], [bo])
            psD, bD = kb.bank()
            for cc in range(2):
                P.op("pe", lambda e, psD=psD, kbt=kbt, vt_=vt_, c=c, cc=cc: e.matmul(
                    psD[:, cc * 256:(cc + 1) * 256], lhsT=kbt[:, cc * 128:(cc + 1) * 128], rhs=vt_[:, c, :],
                    start=True, stop=True), reads=[bkb, bv], writes=[bD], inc=(cc == 1))
            tm, btm = tmp[i2]
            P.op("dve", lambda e, tm=tm, psD=psD: e.tensor_tensor(
                out=tm[:], in0=psD[:, :], in1=S[:, :, :].rearrange("p a d -> p (a d)"), op=ALU.add),
                reads=[bD, bS], writes=[btm])
            for cc in range(2):
                P.op("dve", lambda e, tm=tm, lm=lm, cc=cc: e.tensor_scalar(
                    out=S[:, cc, :], in0=tm[:, cc * 256:(cc + 1) * 256], scalar1=lm[:, 2 * cc:2 * cc + 1], scalar2=None,
                    op0=ALU.mult), reads=[btm, blm], writes=[bS])
                P.op("act", lambda e, tm=tm, lm=lm, cc=cc: e.activation(
                    out=Sb[:, cc, :], in_=tm[:, cc * 256:(cc + 1) * 256], func=AF.Copy, scale=lm[:, 2 * cc:2 * cc + 1]),
                    reads=[btm, blm], writes=[bSb])
        sl = slice(gi * 64 * G, (gi + 1) * 64 * G)
        P.dma("sp", lambda e, ot_=ot_, sl=sl: e.dma_start(
            out=o_o[sl, :].rearrange("(c p) d -> p c d", p=64), in_=ot_[:]), reads=[bo], final=True)
    return kb.finish()


def res_proj(kb, ws, aT, baT, T, w_ap, res_src, res_bufs, dst, dst_bufs, stf, rst):
    P = kb.P
    for nb in range(4):
        def evac(tt, ps, bps, nb=nb):
            r, br = rst.next()
            rd = [res_bufs[tt]] if res_bufs is not None else []
            P.dma("sp", lambda e: e.dma_start(out=r[:], in_=res_src[tt * 128:(tt + 1) * 128, nb * 512:(nb + 1) * 512]),
                  reads=rd, writes=[br])
            s_, bs_ = stf.next()
            P.op("dve", lambda e: e.tensor_tensor(out=s_[:], in0=ps[:, :], in1=r[:], op=ALU.add),
                 reads=[bps, br], writes=[bs_])
            P.dma("sp", lambda e: e.dma_start(out=dst[tt * 128:(tt + 1) * 128, nb * 512:(nb + 1) * 512], in_=s_[:]),
                  reads=[bs_], writes=[dst_bufs[tt]])
        proj_tok(kb, ws, aT, baT, T, w_ap, nb * 512, 512, evac)


def emit_mlp(kb, ws, src, src_bufs, gt, gb, w_up, w_dn, dst, dst_bufs, T, xn5, bxn5, big, stf, rst, rl):
    P = kb.P
    TB = min(512, T)
    nts = TB // 128
    hT = big[:, 0:64 * TB].rearrange("p (f t) -> p f t", f=64)
    bhT = [kb.buf("hT") for _ in range(64)]
    for t0 in range(0, T, TB):
        kb.rows.run(src, t0, nts, gt, gb, src_bufs=src_bufs, dstT=xn5, bdstT=bxn5, col0=0)
        for f4 in range(16):
            def evac(j, tt0, nt, ps, bps, f4=f4):
                r, br = rl.next()
                P.op("act", lambda e: e.activation(out=r[:, 0:nt], in_=ps[:, 0:nt], func=AF.Relu), [bps], [br])
                P.op("dve", lambda e: e.tensor_tensor(out=hT[:, f4 * 4 + j, 0:nt], in0=ps[:, 0:nt], in1=r[:, 0:nt],
                                                      op=ALU.mult), [bps, br], [bhT[f4 * 4 + j]])
            proj_feat(kb, ws, xn5, bxn5, TB, w_up, f4 * 512, 512, evac, tb=TB)
        for nb in range(4):
            banks = [kb.bank() for _ in range(nts)]
            for kq in range(4):
                wt, wb = ws.load(w_dn, kq * 2048, 2048, nb * 512, 512)
                for ts in range(nts):
                    ps, bps = banks[ts]
                    for kc in range(16):
                        f = kq * 16 + kc
                        P.op("pe", lambda e, ps=ps, f=f, ts=ts, kc=kc, wt=wt, kq=kq: e.matmul(
                            ps[:, :], lhsT=hT[:, f, ts * 128:(ts + 1) * 128], rhs=wt[:, kc, :],
                            start=(kq == 0 and kc == 0), stop=(kq == 3 and kc == 15)),
                            reads=[bhT[f], wb], writes=[bps], inc=(kc == 15))
            for ts in range(nts):
                ps, bps = banks[ts]
                r0 = t0 + ts * 128
                tt = r0 // 128
                r, br = rst.next()
                rd = [src_bufs[tt]] if src_bufs is not None else []
                P.dma("sp", lambda e, r=r, r0=r0, nb=nb: e.dma_start(
                    out=r[:], in_=src[r0:r0 + 128, nb * 512:(nb + 1) * 512]), reads=rd, writes=[br])
                s_, bs_ = stf.next()
                P.op("dve", lambda e, s_=s_, ps=ps, r=r: e.tensor_tensor(out=s_[:], in0=ps[:, :], in1=r[:], op=ALU.add),
                     reads=[bps, br], writes=[bs_])
                P.dma("sp", lambda e, s_=s_, r0=r0, nb=nb: e.dma_start(
                    out=dst[r0:r0 + 128, nb * 512:(nb + 1) * 512], in_=s_[:]), reads=[bs_], writes=[dst_bufs[tt]])


def common_setup(kb, ident_d):
    kb.init_psum()
    load_consts(kb, ident_d)
    kb.eps, kb.beps = kb.sbb("eps", [128, 1], F32)
    kb.P.op("dve", lambda e: e.memset(kb.eps[:], EPS), writes=[kb.beps])
    kb.rows = Rows(kb)


class GVec:
    def __init__(self, kb):
        self.kb = kb
        self.t = [kb.sbb("gvec%d" % i, [128, D], F32) for i in range(2)]
        self.i = 0

    def load(self, vec_ap):
        t, b = self.t[self.i % 2]
        self.i += 1
        self.kb.P.dma("sp", lambda e: e.dma_start(out=t[:], in_=vec_ap.partition_broadcast(128)), writes=[b])
        return t, b


def build_C(T):
    kb = KB()
    P = kb.P
    o = kb.inp("o", [T, D])
    sr = kb.inp("sr", [T, D], BF16)
    x = kb.inp("x", [T, D])
    hg4 = kb.inp("hg4", [D])
    w_out = kb.inp("w_out", [D, D])
    g2 = kb.inp("g2", [D])
    w_up = kb.inp("w_up", [D, DFF])
    w_dn = kb.inp("w_dn", [DFF, D])
    gkv = kb.inp("gkv", [D])
    kv_w = kb.inp("kv_w", [D, 1024])
    g1b = kb.inp("g1b", [D])
    w_q = kb.inp("w_q", [D, D])
    ident_d = kb.inp("ident", [128, 128])
    h1, _ = kb.out("h1", [T, D])
    KT_o, _ = kb.out("KT", [4, 128, T], BF16)
    V_o, _ = kb.out("V", [T, 512], BF16)
    QT_o, _ = kb.out("QT", [16, 128, T], BF16)
    hA = kb.dram("hA", [T, D], F32)
    common_setup(kb, ident_d)
    ntt = T // 128
    bhA = [kb.buf("hA") for _ in range(ntt)]
    bh1 = [kb.buf("h1") for _ in range(ntt)]
    big = kb.sb("big", [128, 32768], BF16)
    big3 = big[:, 0:16 * T].rearrange("p (k t) -> p k t", k=16)
    bbig = [kb.buf("big") for _ in range(ntt)]
    TB = min(512, T)
    xn5 = kb.sb("xn5", [128, 16, TB], BF16)
    bxn5 = [kb.buf("xn5") for _ in range(TB // 128)]
    ws = WStream(kb)
    gv = GVec(kb)
    stf = Stager(kb, "stf", [128, 512], F32)
    stb = Stager(kb, "stb", [128, 512], BF16)
    rst = Stager(kb, "rst", [128, 512], F32)
    rl = Stager(kb, "rl", [128, 512], F32, n=2)
    gt, gb = gv.load(hg4)
    kb.rows.run(o, 0, ntt, gt, gb, nheads=4, mul_src=sr, dstT=big3, bdstT=bbig)
    res_proj(kb, ws, big3, bbig, T, w_out, x, None, hA, bhA, stf, rst)
    P.barrier()
    gt, gb = gv.load(g2)
    emit_mlp(kb, ws, hA, bhA, gt, gb, w_up, w_dn, h1, bh1, T, xn5, bxn5, big, stf, rst, rl)
    P.barrier()
    gt, gb = gv.load(gkv)
    kb.rows.run(h1, 0, ntt, gt, gb, src_bufs=bh1, dstT=big3, bdstT=bbig)

    def evacK(j, t0, nt, ps, bps):
        s_, bs_ = stb.next()
        evac_copy(kb, kb.evac_eng(), s_[:, 0:nt], ps[:, 0:nt], [bps], [bs_])
        P.dma("sp", lambda e: e.dma_start(out=KT_o[j, :, t0:t0 + nt], in_=s_[:, 0:nt]), reads=[bs_], final=True)
    proj_feat(kb, ws, big3, bbig, T, kv_w, 0, 512, evacK)

    def evacV(tt, ps, bps):
        s_, bs_ = stb.next()
        evac_copy(kb, kb.evac_eng(), s_[:], ps[:, :], [bps], [bs_])
        P.dma("sp", lambda e: e.dma_start(out=V_o[tt * 128:(tt + 1) * 128, :], in_=s_[:]), reads=[bs_], final=True)
    proj_tok(kb, ws, big3, bbig, T, kv_w, 512, 512, evacV)
    gt, gb = gv.load(g1b)
    kb.rows.run(h1, 0, ntt, gt, gb, src_bufs=bh1, dstT=big3, bdstT=bbig)
    for nb in range(4):
        def evacQ(j, t0, nt, ps, bps, nb=nb):
            s_, bs_ = stb.next()
            evac_copy(kb, kb.evac_eng(), s_[:, 0:nt], ps[:, 0:nt], [bps], [bs_], scale=SB_DH ** -0.5)
            P.dma("sp", lambda e: e.dma_start(out=QT_o[nb * 4 + j, :, t0:t0 + nt], in_=s_[:, 0:nt]), reads=[bs_], final=True)
        proj_feat(kb, ws, big3, bbig, T, w_q, nb * 512, 512, evacQ)
    for b in bh1:
        for k_, v_ in b.w.items():
            if P.final.get(k_, 0) < v_:
                P.final[k_] = v_
    return kb.finish()


def build_E(T):
    kb = KB()
    P = kb.P
    OT = kb.inp("OT", [16, 128, T], BF16)
    h1 = kb.inp("h1", [T, D])
    w_out = kb.inp("w_out", [D, D])
    g2 = kb.inp("g2", [D])
    w_up = kb.inp("w_up", [D, DFF])
    w_dn = kb.inp("w_dn", [DFF, D])
    gf = kb.inp("gf", [D])
    ident_d = kb.inp("ident", [128, 128])
    y, _ = kb.out("y", [T, D])
    hB = kb.dram("hB", [T, D], F32)
    h2 = kb.dram("h2", [T, D], F32)
    common_setup(kb, ident_d)
    ntt = T // 128
    bhB = [kb.buf("hB") for _ in range(ntt)]
    bh2 = [kb.buf("h2") for _ in range(ntt)]
    big = kb.sb("big", [128, 32768], BF16)
    big3 = big[:, 0:16 * T].rearrange("p (k t) -> p k t", k=16)
    bbig = [kb.buf("big") for _ in range(ntt)]
    TB = min(512, T)
    xn5 = kb.sb("xn5", [128, 16, TB], BF16)
    bxn5 = [kb.buf("xn5") for _ in range(TB // 128)]
    ws = WStream(kb)
    gv = GVec(kb)
    stf = Stager(kb, "stf", [128, 512], F32)
    rst = Stager(kb, "rst", [128, 512], F32)
    rl = Stager(kb, "rl", [128, 512], F32, n=2)
    for h in range(16):
        P.dma("sp", lambda e, h=h: e.dma_start(out=big3[:, h, :], in_=OT[h, :, :]), writes=bbig, key=("dma", "OTload"))
    res_proj(kb, ws, big3, bbig, T, w_out, h1, None, hB, bhB, stf, rst)
    P.barrier()
    gt, gb = gv.load(g2)
    emit_mlp(kb, ws, hB, bhB, gt, gb, w_up, w_dn, h2, bh2, T, xn5, bxn5, big, stf, rst, rl)
    gt, gb = gv.load(gf)
    kb.rows.run(h2, 0, ntt, gt, gb, src_bufs=bh2, dst_dram=y, final=True)
    return kb.finish()


def build_D(N):
    kb = KB()
    P = kb.P
    QT2 = kb.inp("QT2", [2, 128, N], BF16)
    KT_d = kb.inp("KT", [128, N], BF16)
    V_d = kb.inp("V", [N, 128], BF16)
    masks_d = kb.inp("masks", [128, 4, 512])
    negU_d = kb.inp("negU", [128, 128], BF16)
    onesb_d = kb.inp("onesb", [128, 128], BF16)
    OT2, _ = kb.out("OT2", [2, 128, N], BF16)
    kb.init_psum()
    obanks = kb.banks[:2]
    kb.banks = kb.banks[2:]
    NB = N // 128
    NQ = N // 512
    KT, bKT = kb.sbb("KT", [128, N], BF16)
    Vt, bVt = kb.sbb("Vt", [128, NB, 128], BF16)
    QT = [kb.sbb("QT%d" % i, [128, N], BF16) for i in range(2)]
    mk, bmk = kb.sbb("mk", [128, 4, 512], F32)
    negU, bnegU = kb.sbb("negU", [128, 128], BF16)
    onesb, bonesb = kb.sbb("onesb", [128, 128], BF16)
    P.dma("sp", lambda e: e.dma_start(out=KT[:], in_=KT_d[:, :]), writes=[bKT])
    P.dma("sp", lambda e: e.dma_start(out=Vt[:], in_=V_d[:, :].rearrange("(b p) d -> p b d", p=128)), writes=[bVt])
    for i in range(2):
        P.dma("sp", lambda e, i=i: e.dma_start(out=QT[i][0][:], in_=QT2[i, :, :]), writes=[QT[i][1]])
    P.dma("sp", lambda e: e.dma_start(out=mk[:], in_=masks_d[:, :, :]), writes=[bmk])
    P.dma("sp", lambda e: e.dma_start(out=negU[:], in_=negU_d[:, :]), writes=[bnegU])
    P.dma("sp", lambda e: e.dma_start(out=onesb[:], in_=onesb_d[:, :]), writes=[bonesb])
    et = Stager(kb, "e", [128, 512], F32, n=2)
    spf = Stager(kb, "spf", [128, 512], F32, n=2)
    spb = Stager(kb, "spb", [128, 512], BF16, n=2)
    lwt = Stager(kb, "lw", [128, 512], F32, n=2)
    wf = Stager(kb, "wf", [128, 512], F32, n=2)
    wb_ = Stager(kb, "wb", [128, 512], BF16, n=2)
    ost = Stager(kb, "ost", [128, 512], BF16, n=2)
    R, bR = kb.sbb("R", [128, 512], F32)
    qi = 0
    for h in range(2):
        Qh, bQh = QT[h]
        for qt in range(NQ):
            psO, bO = obanks[qi % 2]
            qi += 1
            qs = slice(qt * 512, (qt + 1) * 512)
            first = True
            for kb_ in range(4 * qt + 3, -1, -1):
                d = kb_ - 4 * qt
                ks = slice(kb_ * 128, (kb_ + 1) * 128)
                psS, bS_ = kb.bank()
                P.op("pe", lambda e, psS=psS, ks=ks, qs=qs, Qh=Qh: e.matmul(
                    psS[:, :], lhsT=KT[:, ks], rhs=Qh[:, qs], start=True, stop=True),
                    reads=[bKT, bQh], writes=[bS_])
                e_, be = et.next()
                P.op("act", lambda e, e_=e_, psS=psS: e.activation(out=e_[:], in_=psS[:, :], func=AF.Exp),
                     reads=[bS_], writes=[be])
                sb_, bsb = spb.next()
                if d >= 0:
                    sf, bsf = spf.next()
                    P.op("act", lambda e, sf=sf, e_=e_: e.activation(out=sf[:], in_=e_[:], func=AF.Ln, bias=1.0),
                         reads=[be], writes=[bsf])
                    P.op("dve", lambda e, sb_=sb_, sf=sf, d=d: e.tensor_tensor(
                        out=sb_[:], in0=sf[:], in1=mk[:, d, :], op=ALU.mult), reads=[bsf, bmk], writes=[bsb])
                else:
                    P.op("act", lambda e, sb_=sb_, e_=e_: e.activation(out=sb_[:], in_=e_[:], func=AF.Ln, bias=1.0),
                         reads=[be], writes=[bsb])
                psL, bL = kb.bank()
                P.op("pe", lambda e, psL=psL, ks=ks, qs=qs, Qh=Qh: e.matmul(
                    psL[:, :], lhsT=KT[:, ks], rhs=Qh[:, qs], start=True, stop=False),
                    reads=[bKT, bQh], writes=[bL], inc=False)
                P.op("pe", lambda e, psL=psL, sb_=sb_: e.matmul(
                    psL[:, :], lhsT=negU[:, :], rhs=sb_[:], start=False, stop=True),
                    reads=[bnegU, bsb], writes=[bL])
                psT, bT = kb.bank()
                P.op("pe", lambda e, psT=psT, sb_=sb_: e.matmul(
                    psT[:, :], lhsT=onesb[:, :], rhs=sb_[:], start=True, stop=True),
                    reads=[bonesb, bsb], writes=[bT])
                lw, blw = lwt.next()
                if first:
                    P.op("dve", lambda e, lw=lw, psL=psL: e.tensor_copy(out=lw[:], in_=psL[:, :]), reads=[bL], writes=[blw])
                    P.op("dve", lambda e, psT=psT: e.tensor_copy(out=R[:], in_=psT[:, :]), reads=[bT], writes=[bR])
                else:
                    P.op("dve", lambda e, lw=lw, psL=psL: e.tensor_tensor(
                        out=lw[:], in0=psL[:, :], in1=R[:], op=ALU.subtract), reads=[bL, bR], writes=[blw])
                    if kb_ > 0:
                        P.op("dve", lambda e, psT=psT: e.tensor_tensor(
                            out=R[:], in0=psT[:, :], in1=R[:], op=ALU.add), reads=[bT, bR], writes=[bR])
                w_, bw = wb_.next()
                if d >= 0:
                    wf_, bwf = wf.next()
                    P.op("act", lambda e, wf_=wf_, lw=lw: e.activation(out=wf_[:], in_=lw[:], func=AF.Exp),
                         reads=[blw], writes=[bwf])
                    P.op("dve", lambda e, w_=w_, wf_=wf_, d=d: e.tensor_tensor(
                        out=w_[:], in0=wf_[:], in1=mk[:, d, :], op=ALU.mult), reads=[bwf, bmk], writes=[bw])
                else:
                    P.op("act", lambda e, w_=w_, lw=lw: e.activation(out=w_[:], in_=lw[:], func=AF.Exp),
                         reads=[blw], writes=[bw])
                P.op("pe", lambda e, psO=psO, w_=w_, kb_=kb_, first=first: e.matmul(
                    psO[:, :], lhsT=Vt[:, kb_, :], rhs=w_[:], start=first, stop=(kb_ == 0)),
                    reads=[bVt, bw], writes=[bO])
                first = False
            os_, bos = ost.next()
            evac_copy(kb, "dve", os_[:], psO[:, :], [bO], [bos])
            P.dma("sp", lambda e, os_=os_, h=h, qs=qs: e.dma_start(out=OT2[h, :, qs], in_=os_[:]), reads=[bos], final=True)
    return kb.finish()


_CACHE = {}


def _get(name, fn, *a):
    key = (name,) + a
    if key not in _CACHE:
        _CACHE[key] = fn(*a)
    return _CACHE[key]


def _run(nc, in_maps):
    res = run_bass_kernel_spmd(nc, in_maps, core_ids=list(range(NCORES)))
    return res.results


def attn_consts():
    import ml_dtypes
    j = np.arange(128)[:, None, None]
    d = np.arange(4)[None, :, None]
    i = np.arange(512)[None, None, :]
    masks = (j + 128 * d < i).astype(np.float32)
    negU = (-(np.arange(128)[:, None] >= np.arange(128)[None, :]).astype(np.float32)).astype(ml_dtypes.bfloat16)
    onesb = np.ones((128, 128), np.float32).astype(ml_dtypes.bfloat16)
    return masks, negU, onesb


def kernel(x, norm1_g, norm2_g, gla_w_in, gla_w_gate_up, gla_b_gate, gla_head_g, gla_w_out, kv_norm_g, kv_w,
           sb_w_q, sb_w_out, mlp_w_up, mlp_w_down, final_g):
    import ml_dtypes
    f = lambda a: np.ascontiguousarray(np.asarray(a, dtype=np.float32))
    x2 = f(x)[0]
    S = x2.shape[0]
    T = S // NCORES
    ident = np.eye(128, dtype=np.float32)
    cs = [slice(c * T, (c + 1) * T) for c in range(NCORES)]
    w_in = f(gla_w_in[0]); wgu = f(gla_w_gate_up[0]); bg = f(gla_b_gate[0]); g1 = f(norm1_g[0])
    rA = _run(_get("A", build_A, T), [{"x": f(x2[cs[c]]), "g1": g1, "w_in": w_in, "wgu": wgu, "bg": bg, "ident": ident}
                                      for c in range(NCORES)])
    cat = lambda n: np.concatenate([np.asarray(rA[c][n]) for c in range(NCORES)], 0)
    qf, kf, gf_, vf, srf = cat("q"), cat("k"), cat("g"), cat("v"), cat("sr")
    tri = np.triu(np.ones((64, 64), np.float32))
    identb = np.eye(64, dtype=np.float32).astype(ml_dtypes.bfloat16)
    ones = np.ones((64, 2), np.float32)
    mapsB = []
    for c in range(NCORES):
        h, d = c // 2, c % 2
        hs = slice(h * 256, (h + 1) * 256)
        mapsB.append({"q": f(qf[:, hs]), "k": f(kf[:, hs]), "g": f(gf_[:, hs]),
                      "v": np.ascontiguousarray(vf[:, h * 512 + d * 256: h * 512 + (d + 1) * 256]),
                      "tri": tri, "identb": identb, "ones": ones})
    rB = _run(_get("B", build_B, S), mapsB)
    of = np.empty((S, D), np.float32)
    for c in range(NCORES):
        h, d = c // 2, c % 2
        of[:, h * 512 + d * 256: h * 512 + (d + 1) * 256] = np.asarray(rB[c]["o"])
    hg4 = np.ascontiguousarray(np.tile(f(gla_head_g[0]), 4))
    cst = {"hg4": hg4, "w_out": f(gla_w_out[0]), "g2": f(norm2_g[0]), "w_up": f(mlp_w_up[0]), "w_dn": f(mlp_w_down[0]),
           "gkv": f(kv_norm_g), "kv_w": f(kv_w), "g1b": f(norm1_g[1]), "w_q": f(sb_w_q[0]), "ident": ident}
    rC = _run(_get("C", build_C, T), [dict(cst, o=f(of[cs[c]]), sr=np.ascontiguousarray(srf[cs[c]]), x=f(x2[cs[c]]))
                                      for c in range(NCORES)])
    h1 = [np.asarray(rC[c]["h1"]) for c in range(NCORES)]
    KTf = np.concatenate([np.asarray(rC[c]["KT"]) for c in range(NCORES)], 2)
    Vf = np.concatenate([np.asarray(rC[c]["V"]) for c in range(NCORES)], 0)
    QTf = np.concatenate([np.asarray(rC[c]["QT"]) for c in range(NCORES)], 2)
    masks, negU, onesb = attn_consts()
    mapsD = []
    for c in range(NCORES):
        kvh = c // 2
        mapsD.append({"QT2": np.ascontiguousarray(QTf[2 * c:2 * c + 2]), "KT": np.ascontiguousarray(KTf[kvh]),
                      "V": np.ascontiguousarray(Vf[:, kvh * 128:(kvh + 1) * 128]),
                      "masks": masks, "negU": negU, "onesb": onesb})
    rD = _run(_get("D", build_D, S), mapsD)
    OTf = np.concatenate([np.asarray(rD[c]["OT2"]) for c in range(NCORES)], 0)
    cst = {"w_out": f(sb_w_out[0]), "g2": f(norm2_g[1]), "w_up": f(mlp_w_up[1]), "w_dn": f(mlp_w_down[1]),
           "gf": f(final_g), "ident": ident}
    rE = _run(_get("E", build_E, T), [dict(cst, OT=np.ascontiguousarray(OTf[:, :, cs[c]]), h1=h1[c]) for c in range(NCORES)])
    y = np.concatenate([np.asarray(rE[c]["y"]) for c in range(NCORES)], 0)
    return y.reshape(1, S, D).astype(np.float32)
```

```python
import numpy as np
from contextlib import ExitStack
import concourse.bass as bass
import concourse.mybir as mybir
from concourse.bass_utils import run_bass_kernel_spmd

F32 = mybir.dt.float32
BF16 = mybir.dt.bfloat16
AF = mybir.ActivationFunctionType
ALU = mybir.AluOpType

NCORES = 8
D = 2048
SEQ = 16384
DFF = 8192
EPS = 1e-6
GLA_H, GLA_DK, GLA_DV = 4, 256, 512
GLA_IN = 6160
SB_H, SB_DH, SB_KVH = 16, 128, 4
ENGS = ("pe", "act", "dve", "pool", "sp")


class Buf:
    __slots__ = ("name", "w", "r")

    def __init__(self, name):
        self.name = name
        self.w = {}
        self.r = {}


class Prog:
    def __init__(self, nc):
        self.nc = nc
        self.q = {e: [] for e in ENGS}
        self.cnt = {}
        self.sems = {}
        self.seen = {e: {} for e in ENGS}
        self._stack = []
        self.final = {}

    def _sem(self, key):
        if key not in self.sems:
            cm = self.nc.semaphore("s%d" % len(self.sems))
            self.sems[key] = cm.__enter__()
            self._stack.append(cm)
            self.cnt[key] = 0
        return self.sems[key]

    def close(self):
        for cm in reversed(self._stack):
            cm.__exit__(None, None, None)

    def _deps(self, eng, reads, writes):
        need = {}
        for b in reads:
            for k, v in b.w.items():
                if need.get(k, 0) < v:
                    need[k] = v
        for b in writes:
            for d in (b.w, b.r):
                for k, v in d.items():
                    if need.get(k, 0) < v:
                        need[k] = v
        out = []
        seen = self.seen[eng]
        for k, v in need.items():
            if k == eng and eng == "pe":
                continue
            if seen.get(k, 0) >= v:
                continue
            seen[k] = v
            out.append((k, v))
        return out

    def _record(self, ev, reads, writes):
        k, v = ev
        for b in reads:
            if b.r.get(k, 0) < v:
                b.r[k] = v
        for b in writes:
            if b.w.get(k, 0) < v:
                b.w[k] = v

    def op(self, eng, fn, reads=(), writes=(), inc=True):
        waits = self._deps(eng, reads, writes)
        sem = self._sem(eng)
        if inc:
            self.cnt[eng] += 1
        ev = (eng, self.cnt[eng] if inc else self.cnt[eng] + 1)
        self._record(ev, reads, writes)
        sems = self.sems

        def run(e):
            for k, v in waits:
                e.wait_ge(sems[k], v)
            ins = fn(e)
            if inc:
                ins.then_inc(sem, 1)
        self.q[eng].append(run)

    def dma(self, eng, fn, reads=(), writes=(), key=None, final=False):
        waits = self._deps(eng, reads, writes)
        if key is None:
            key = ("dma", (writes[0] if writes else reads[0]).name)
        sem = self._sem(key)
        self.cnt[key] += 16
        ev = (key, self.cnt[key])
        self._record(ev, reads, writes)
        if final:
            self.final[key] = self.cnt[key]
        sems = self.sems

        def run(e):
            for k, v in waits:
                e.wait_ge(sems[k], v)
            fn(e).then_inc(sem, 16)
        self.q[eng].append(run)

    def coll(self, fn, reads=(), writes=(), key=None):
        eng = "pool"
        waits = self._deps(eng, reads, writes)
        sem = self._sem(key)
        self.cnt[key] += 1
        ev = (key, self.cnt[key])
        self._record(ev, reads, writes)
        sems = self.sems

        def run(e):
            for k, v in waits:
                e.wait_ge(sems[k], v)
            fn(e).then_inc(sem)
        self.q[eng].append(run)

    def barrier(self):
        snap = dict(self.cnt)
        sems = self.sems
        for eng in ENGS:
            waits = []
            seen = self.seen[eng]
            for k, v in snap.items():
                if v == 0 or k == eng or seen.get(k, 0) >= v:
                    continue
                seen[k] = v
                waits.append((k, v))

            def run(e, waits=waits):
                for k, v in waits:
                    e.wait_ge(sems[k], v)
            self.q[eng].append(run)

    def final_wait(self, eng):
        waits = list(self.final.items())
        sems = self.sems

        def run(e):
            for k, v in waits:
                e.wait_ge(sems[k], v)
        self.q[eng].append(run)

    def emit(self):
        nc = self.nc
        q = self.q
        with nc.Block() as block:
            @block.tensor
            def _(e):
                for f in q["pe"]:
                    f(e)

            @block.scalar
            def _(e):
                for f in q["act"]:
                    f(e)

            @block.vector
            def _(e):
                for f in q["dve"]:
                    f(e)

            @block.gpsimd
            def _(e):
                for f in q["pool"]:
                    f(e)

            @block.sync
            def _(e):
                for f in q["sp"]:
                    f(e)


class KB:
    def __init__(self):
        self.nc = bass.Bass("TRN2", target_bir_lowering=False)
        self.P = Prog(self.nc)
        self.es = ExitStack()
        self.nbuf = 0
        self.banks = []
        self.bi = 0
        self.outs = []
        self.rr = 0

    def dram(self, name, shape, dt, kind="Internal"):
        return self.nc.dram_tensor(name, list(shape), dt, kind=kind).ap()

    def inp(self, name, shape, dt=F32):
        return self.dram(name, shape, dt, "ExternalInput")

    def out(self, name, shape, dt=F32):
        ap = self.dram(name, shape, dt, "ExternalOutput")
        return ap, None

    def sb(self, name, shape, dt):
        return self.es.enter_context(self.nc.sbuf_tensor("sb_" + name, list(shape), dt))

    def buf(self, name="b"):
        self.nbuf += 1
        return Buf("%s_%d" % (name, self.nbuf))

    def sbb(self, name, shape, dt):
        return self.sb(name, shape, dt), self.buf(name)

    def init_psum(self, n=8):
        for i in range(n):
            t = self.es.enter_context(self.nc.psum_tensor("ps%d" % i, [128, 512], F32))
            self.banks.append((t, self.buf("ps%d" % i)))

    def bank(self):
        r = self.banks[self.bi % len(self.banks)]
        self.bi += 1
        return r

    def evac_eng(self):
        self.rr += 1
        return "act" if self.rr % 2 else "dve"

    def finish(self):
        self.P.final_wait("sp")
        self.P.emit()
        self.es.close()
        self.P.close()
        return self.nc


def evac_copy(kb, eng, out_ap, in_ap, reads, writes, scale=None):
    P = kb.P
    if eng == "act":
        if scale is None:
            P.op("act", lambda e: e.activation(out=out_ap, in_=in_ap, func=AF.Copy), reads, writes)
        else:
            P.op("act", lambda e: e.activation(out=out_ap, in_=in_ap, func=AF.Copy, scale=float(scale)), reads, writes)
    else:
        if scale is None:
            P.op(eng, lambda e: e.tensor_copy(out=out_ap, in_=in_ap), reads, writes)
        else:
            P.op(eng, lambda e: e.tensor_scalar(out=out_ap, in0=in_ap, scalar1=float(scale), scalar2=None,
                                                 op0=ALU.mult), reads, writes)


def load_consts(kb, ident_d):
    ident, bident = kb.sbb("ident", [128, 128], F32)
    kb.P.dma("sp", lambda e: e.dma_start(out=ident[:], in_=ident_d[:, :]), writes=[bident])
    kb.ident, kb.bident = ident, bident


def load_bcast(kb, name, vec_ap, n):
    t, b = kb.sbb(name, [128, n], F32)
    kb.P.dma("sp", lambda e: e.dma_start(out=t[:], in_=vec_ap.partition_broadcast(128)), writes=[b])
    return t, b


class Rows:
    def __init__(self, kb):
        self.kb = kb
        self.xin = [kb.sbb("r_xin%d" % i, [128, D], F32) for i in range(2)]
        self.xn = [kb.sbb("r_xn%d" % i, [128, D], F32) for i in range(2)]
        self.st = [kb.sbb("r_st%d" % i, [128, 12], F32) for i in range(2)]
        self.mul = None
        self.i = 0

    def run(self, src, row0, ntiles, gt, gb, src_bufs=None, nheads=1, mul_src=None,
            dstT=None, bdstT=None, col0=0, dst_dram=None, dst_bufs=None, final=False):
        kb = self.kb
        P = kb.P
        W = D // nheads
        if mul_src is not None and self.mul is None:
            self.mul = [kb.sbb("r_mul%d" % i, [128, D], BF16) for i in range(2)]
        for ti in range(ntiles):
            r0 = row0 + ti * 128
            tt = r0 // 128
            i2 = self.i % 2
            self.i += 1
            xt, bx = self.xin[i2]
            xnt, bxn = self.xn[i2]
            s, bs = self.st[i2]
            rd = [src_bufs[tt]] if src_bufs is not None else []
            P.dma("sp", lambda e, xt=xt, r0=r0: e.dma_start(out=xt[:], in_=src[r0:r0 + 128, :]), reads=rd, writes=[bx])
            if mul_src is not None:
                mt, bm = self.mul[i2]
                P.dma("sp", lambda e, mt=mt, r0=r0: e.dma_start(out=mt[:], in_=mul_src[r0:r0 + 128, :]), writes=[bm])
            for h in range(nheads):
                P.op("act", lambda e, xt=xt, xnt=xnt, s=s, h=h: e.activation(
                    out=xnt[:, h * W:(h + 1) * W], in_=xt[:, h * W:(h + 1) * W], func=AF.Square, accum_out=s[:, h:h + 1]),
                    reads=[bx], writes=[bxn, bs])
            P.op("act", lambda e, s=s: e.activation(out=s[:, 4:4 + nheads], in_=s[:, 0:nheads], func=AF.Sqrt, scale=1.0 / W,
                                                    bias=kb.eps[:, 0:1]), reads=[bs, kb.beps], writes=[bs])
            P.op("dve", lambda e, s=s: e.reciprocal(out=s[:, 8:8 + nheads], in_=s[:, 4:4 + nheads]), reads=[bs], writes=[bs])
            for h in range(nheads):
                P.op("dve", lambda e, xt=xt, xnt=xnt, s=s, h=h: e.scalar_tensor_tensor(
                    out=xnt[:, h * W:(h + 1) * W], in0=xt[:, h * W:(h + 1) * W], scalar=s[:, 8 + h:9 + h],
                    in1=gt[:, h * W:(h + 1) * W], op0=ALU.mult, op1=ALU.mult),
                    reads=[bx, bs, gb], writes=[bxn])
            if mul_src is not None:
                P.op("dve", lambda e, xnt=xnt, mt=mt: e.tensor_tensor(out=xnt[:], in0=xnt[:], in1=mt[:], op=ALU.mult),
                     reads=[bxn, bm], writes=[bxn])
            if dstT is not None:
                c = col0 + ti * 128
                for b4 in range(4):
                    ps, bps = kb.bank()
                    for cc in range(4):
                        kc = b4 * 4 + cc
                        P.op("pe", lambda e, ps=ps, xnt=xnt, kc=kc, cc=cc: e.transpose(
                            ps[:, cc * 128:(cc + 1) * 128], xnt[:, kc * 128:(kc + 1) * 128], kb.ident[:]),
                            reads=[bxn, kb.bident], writes=[bps], inc=(cc == 3))
                    evac_copy(kb, kb.evac_eng(), dstT[:, b4 * 4:(b4 + 1) * 4, c:c + 128],
                              ps[:, :].rearrange("p (c t) -> p c t", c=4), [bps], [bdstT[c // 128]])
            if dst_dram is not None:
                wr = [dst_bufs[tt]] if dst_bufs is not None else []
                P.dma("sp", lambda e, xnt=xnt, r0=r0: e.dma_start(out=dst_dram[r0:r0 + 128, :], in_=xnt[:]),
                      reads=[bxn], writes=wr, final=final)


def emit_norm_T(kb, src, T, gt, gb, xnT, bxnT, nm):
    if not hasattr(kb, "rows"):
        kb.rows = Rows(kb)
    kb.rows.run(src, 0, T // 128, gt, gb, dstT=xnT, bdstT=bxnT)


class WStream:
    def __init__(self, kb, nbuf=2):
        self.kb = kb
        self.t = [kb.sbb("wblk%d" % i, [128, 16, 512], BF16) for i in range(nbuf)]
        self.i = 0

    def load(self, w_ap, r0, nrows, c0, ncols):
        kb = self.kb
        t, b = self.t[self.i % len(self.t)]
        self.i += 1
        nk = nrows // 128
        src = w_ap[r0:r0 + nrows, c0:c0 + ncols].rearrange("(kc p) n -> p kc n", p=128)
        kb.P.dma("pool", lambda e: e.dma_start(out=t[:, 0:nk, 0:ncols], in_=src), writes=[b])
        return t, b


def proj_tok(kb, ws, xnT, bxnT, T, w_ap, c0, ncols, evac, nk=16, r0=0):
    P = kb.P
    wt, wb = ws.load(w_ap, r0, nk * 128, c0, ncols)
    for tt in range(T // 128):
        ps, bps = kb.bank()
        for kc in range(nk):
            P.op("pe", lambda e, ps=ps, kc=kc, tt=tt: e.matmul(
                ps[:, 0:ncols], lhsT=xnT[:, kc, tt * 128:(tt + 1) * 128], rhs=wt[:, kc, 0:ncols],
                start=(kc == 0), stop=(kc == nk - 1)),
                reads=[bxnT[tt], wb], writes=[bps], inc=(kc == nk - 1))
        evac(tt, ps, bps)


def proj_feat(kb, ws, xnT, bxnT, T, w_ap, c0, ncols, evac, nk=16, tb=512, tok0=0):
    P = kb.P
    wt, wb = ws.load(w_ap, 0, nk * 128, c0, ncols)
    for t0 in range(0, T, tb):
        nt = min(tb, T - t0)
        rd = [bxnT[(tok0 + t0) // 128 + i] for i in range(nt // 128)]
        for j in range((ncols + 127) // 128):
            m = min(128, ncols - j * 128)
            ps, bps = kb.bank()
            for kc in range(nk):
                P.op("pe", lambda e, ps=ps, kc=kc, j=j, m=m, t0=t0, nt=nt: e.matmul(
                    ps[0:m, 0:nt], lhsT=wt[:, kc, j * 128:j * 128 + m], rhs=xnT[:, kc, tok0 + t0:tok0 + t0 + nt],
                    start=(kc == 0), stop=(kc == nk - 1)),
                    reads=rd + [wb], writes=[bps], inc=(kc == nk - 1))
            evac(j, t0, nt, ps, bps)


class Stager:
    def __init__(self, kb, name, shape, dt, n=3):
        self.kb = kb
        self.t = [kb.sbb(name + str(i), shape, dt) for i in range(n)]
        self.i = 0

    def next(self):
        r = self.t[self.i % len(self.t)]
        self.i += 1
        return r


def build_A(T):
    kb = KB()
    P = kb.P
    x = kb.inp("x", [T, D])
    g1 = kb.inp("g1", [D])
    w_in = kb.inp("w_in", [D, GLA_IN])
    wgu = kb.inp("wgu", [16, 1024])
    bg = kb.inp("bg", [1024])
    ident_d = kb.inp("ident", [128, 128])
    q_o, bq_o = kb.out("q", [T, 1024])
    k_o, bk_o = kb.out("k", [T, 1024])
    g_o, bg_o = kb.out("g", [T, 1024])
    v_o, bv_o = kb.out("v", [T, 2048], BF16)
    sr_o, bsr_o = kb.out("sr", [T, 2048], BF16)
    kb.init_psum()
    load_consts(kb, ident_d)
    kb.eps, kb.beps = kb.sbb("eps", [128, 1], F32)
    P.op("dve", lambda e: e.memset(kb.eps[:], EPS), writes=[kb.beps])
    gt, gb = load_bcast(kb, "g1b", g1, D)
    bgt, bgb = load_bcast(kb, "bgb", bg, 1024)
    wgu_t, wgu_b = kb.sbb("wgu", [16, 1024], F32)
    P.dma("sp", lambda e: e.dma_start(out=wgu_t[:], in_=wgu[:, :]), writes=[wgu_b])
    xnT = kb.sb("xnT", [128, 16, T], BF16)
    bxnT = [kb.buf("xnT") for _ in range(T // 128)]
    emit_norm_T(kb, x, T, gt, gb, xnT, bxnT, "n1")
    ws = WStream(kb)
    stf = Stager(kb, "stf", [128, 512], F32)
    stb = Stager(kb, "stb", [128, 512], BF16)

    def mk_evac(dst, bdst, col, kind, scale=None):
        def evac(tt, ps, bps):
            if kind == "f32":
                s, bs = stf.next()
                evac_copy(kb, kb.evac_eng(), s[:], ps[:, :], [bps], [bs], scale=scale)
            elif kind == "bf16":
                s, bs = stb.next()
                evac_copy(kb, kb.evac_eng(), s[:], ps[:, :], [bps], [bs])
            else:
                s, bs = stb.next()
                P.op("act", lambda e: e.activation(out=s[:], in_=ps[:, :], func=AF.Silu), [bps], [bs])
            P.dma("sp", lambda e: e.dma_start(out=dst[tt * 128:(tt + 1) * 128, col:col + 512], in_=s[:]),
                  reads=[bs], final=True)
        return evac

    for nb in range(2):
        proj_tok(kb, ws, xnT, bxnT, T, w_in, nb * 512, 512, mk_evac(q_o, bq_o, nb * 512, "f32", GLA_DK ** -0.5))
    for nb in range(2):
        proj_tok(kb, ws, xnT, bxnT, T, w_in, 1024 + nb * 512, 512, mk_evac(k_o, bk_o, nb * 512, "f32"))
    for nb in range(4):
        proj_tok(kb, ws, xnT, bxnT, T, w_in, 2048 + nb * 512, 512, mk_evac(v_o, bv_o, nb * 512, "bf16"))
    alT, balT = kb.sbb("alT", [16, T], F32)
    def evac_al(j, t0, nt, ps, bps):
        evac_copy(kb, "dve", alT[:, t0:t0 + nt], ps[0:16, 0:nt], [bps], [balT])
    proj_feat(kb, ws, xnT, bxnT, T, w_in, 6144, 16, evac_al)
    ez = [kb.sbb("ez%d" % i, [128, 512], F32) for i in range(2)]
    for tt in range(T // 128):
        for hb in range(2):
            ps, bps = kb.bank()
            P.op("pe", lambda e, ps=ps, tt=tt, hb=hb: e.matmul(
                ps[:, :], lhsT=alT[:, tt * 128:(tt + 1) * 128], rhs=wgu_t[:, hb * 512:(hb + 1) * 512],
                start=True, stop=True), reads=[balT, wgu_b], writes=[bps])
            z, bz = ez[(tt * 2 + hb) % 2]
            P.op("dve", lambda e, z=z, ps=ps, hb=hb: e.tensor_tensor(
                out=z[:], in0=ps[:, :], in1=bgt[:, hb * 512:(hb + 1) * 512], op=ALU.add),
                reads=[bps, bgb], writes=[bz])
            P.op("act", lambda e, z=z: e.activation(out=z[:], in_=z[:], func=AF.Exp, scale=-1.0), [bz], [bz])
            s, bs = stf.next()
            P.op("act", lambda e, z=z, s=s: e.activation(out=s[:], in_=z[:], func=AF.Ln, bias=1.0), [bz], [bs])
            P.op("dve", lambda e, s=s: e.tensor_scalar(out=s[:], in0=s[:], scalar1=-1.0 / 16.0, scalar2=None,
                                                         op0=ALU.mult), [bs], [bs])
            P.dma("sp", lambda e, s=s, tt=tt, hb=hb: e.dma_start(
                out=g_o[tt * 128:(tt + 1) * 128, hb * 512:(hb + 1) * 512], in_=s[:]), reads=[bs], final=True)
    for nb in range(4):
        proj_tok(kb, ws, xnT, bxnT, T, w_in, 4096 + nb * 512, 512, mk_evac(sr_o, bsr_o, nb * 512, "silu"))
    return kb.finish()


def build_B(N):
    kb = KB()
    P = kb.P
    q = kb.inp("q", [N, 256])
    k = kb.inp("k", [N, 256])
    g = kb.inp("g", [N, 256])
    v = kb.inp("v", [N, 256], BF16)
    tri_d = kb.inp("tri", [64, 64])
    idb_d = kb.inp("identb", [64, 64], BF16)
    ones_d = kb.inp("ones", [64, 2])
    o_o, _ = kb.out("o", [N, 256])
    kb.init_psum()
    tri, btri = kb.sbb("tri", [64, 64], F32)
    idb, bidb = kb.sbb("idb", [64, 64], BF16)
    ones, bones = kb.sbb("ones", [64, 2], F32)
    P.dma("sp", lambda e: e.dma_start(out=tri[:], in_=tri_d[:, :]), writes=[btri])
    P.dma("sp", lambda e: e.dma_start(out=idb[:], in_=idb_d[:, :]), writes=[bidb])
    P.dma("sp", lambda e: e.dma_start(out=ones[:], in_=ones_d[:, :]), writes=[bones])
    S, bS = kb.sbb("S", [128, 2, 256], F32)
    Sb, bSb = kb.sbb("Sb", [128, 2, 256], BF16)
    P.op("dve", lambda e: e.memset(S[:], 0.0), writes=[bS])
    P.op("dve", lambda e: e.memset(Sb[:], 0.0), writes=[bSb])
    G = 8
    qg = [kb.sbb("qg%d" % i, [64, G, 256], F32) for i in range(2)]
    kg = [kb.sbb("kg%d" % i, [64, G, 256], F32) for i in range(2)]
    gg = [kb.sbb("gg%d" % i, [64, G, 256], F32) for i in range(2)]
    vg = [kb.sbb("vg%d" % i, [64, G, 256], BF16) for i in range(2)]
    og = [kb.sbb("og%d" % i, [64, G, 256], F32) for i in range(2)]
    NBUF = 3
    ecum = [kb.sbb("ecum%d" % i, [64, 256], F32) for i in range(NBUF)]
    encum = [kb.sbb("encum%d" % i, [64, 256], F32) for i in range(NBUF)]
    qtb = [kb.sbb("qtb%d" % i, [64, 256], BF16) for i in range(NBUF)]
    ktb = [kb.sbb("ktb%d" % i, [64, 256], BF16) for i in range(NBUF)]
    qkT = [kb.sbb("qkT%d" % i, [128, 4, 64], BF16) for i in range(NBUF)]
    sTb = [kb.sbb("sTb%d" % i, [64, 64], BF16) for i in range(NBUF)]
    lam = [kb.sbb("lam%d" % i, [128, 4], F32) for i in range(NBUF)]
    tmp = [kb.sbb("tmp%d" % i, [128, 512], F32) for i in range(2)]
    ngrp = N // (64 * G)
    nch = ngrp * G

    def load_group(gi):
        sl = slice(gi * 64 * G, (gi + 1) * 64 * G)
        for (tl, src) in ((qg, q), (kg, k), (gg, g), (vg, v)):
            t, b = tl[gi % 2]
            P.dma("sp", lambda e, t=t, src=src: e.dma_start(
                out=t[:], in_=src[sl, :].rearrange("(c p) d -> p c d", p=64)), writes=[b])

    def grp(n):
        gi, c = n // G, n % G
        return gi, c, qg[gi % 2], kg[gi % 2], gg[gi % 2], vg[gi % 2], og[gi % 2]

    cx = [None] * nch

    def stage1(n):
        gi, c, (qt_, bq), (kt_, bk), (gt_, bgg), (vt_, bv), (ot_, bo) = grp(n)
        if c == 2 and gi + 1 < ngrp:
            load_group(gi + 1)
        i3 = n % NBUF
        psA, bA = kb.bank()
        P.op("pe", lambda e: e.matmul(psA[0:64, 0:256], lhsT=tri[:, :], rhs=gt_[:, c, :], start=True, stop=True),
             reads=[btri, bgg], writes=[bA], inc=False)
        for cc in range(2):
            P.op("pe", lambda e, cc=cc: e.matmul(
                psA[:, 256 + 2 * cc:258 + 2 * cc], lhsT=gt_[:, c, cc * 128:(cc + 1) * 128], rhs=ones[:, 0:2],
                start=True, stop=True), reads=[bgg, bones], writes=[bA], inc=(cc == 1))
        lm, blm = lam[i3]
        P.op("act", lambda e: e.activation(out=lm[:, 0:4], in_=psA[:, 256:260], func=AF.Exp), reads=[bA], writes=[blm])
        ec, bec = ecum[i3]
        en, ben = encum[i3]
        P.op("act", lambda e: e.activation(out=ec[:], in_=psA[0:64, 0:256], func=AF.Exp), reads=[bA], writes=[bec])
        P.op("act", lambda e: e.activation(out=en[:], in_=psA[0:64, 0:256], func=AF.Exp, scale=-1.0), reads=[bA], writes=[ben])
        qb, bqb = qtb[i3]
        kbt, bkb = ktb[i3]
        P.op("dve", lambda e: e.tensor_tensor(out=qb[:], in0=qt_[:, c, :], in1=ec[:], op=ALU.mult),
             reads=[bq, bec], writes=[bqb])
        P.op("dve", lambda e: e.tensor_tensor(out=kbt[:], in0=kt_[:, c, :], in1=en[:], op=ALU.mult),
             reads=[bk, ben], writes=[bkb])
        cx[n] = dict(lm=lm, blm=blm, qb=qb, bqb=bqb, kbt=kbt, bkb=bkb)

    def stage2(n):
        x_ = cx[n]
        qb, bqb, kbt, bkb = x_["qb"], x_["bqb"], x_["kbt"], x_["bkb"]
        i3 = n % NBUF
        psT, bT = kb.bank()
        for idx, (src, bsrc) in enumerate(((qb, bqb), (qb, bqb), (kbt, bkb), (kbt, bkb))):
            cc = idx % 2
            P.op("pe", lambda e, src=src, cc=cc, idx=idx: e.matmul(
                psT[:, idx * 64:(idx + 1) * 64], lhsT=src[:, cc * 128:(cc + 1) * 128], rhs=idb[:, :],
                start=True, stop=True), reads=[bsrc, bidb], writes=[bT], inc=(idx == 3))
        qk, bqk = qkT[i3]
        evac_copy(kb, "act", qk[:, :, :], psT[:, 0:256].rearrange("p (a t) -> p a t", a=4), [bT], [bqk])
        for cc in range(2):
            P.op("pe", lambda e, cc=cc: e.matmul(
                psT[0:64, 256:320], lhsT=qk[:, 2 + cc, :], rhs=qk[:, cc, :], start=(cc == 0), stop=(cc == 1)),
                reads=[bqk], writes=[bT], inc=(cc == 1))
        st, bst = sTb[i3]
        P.op("dve", lambda e: e.tensor_tensor(out=st[:], in0=psT[0:64, 256:320], in1=tri[:, :], op=ALU.mult),
             reads=[bT, btri], writes=[bst])
        x_.update(qk=qk, bqk=bqk, st=st, bst=bst)

    def stage3(n):
        gi, c, (qt_, bq), (kt_, bk), (gt_, bgg), (vt_, bv), (ot_, bo) = grp(n)
        x_ = cx[n]
        qk, bqk, st, bst, kbt, bkb, lm, blm = (x_[k_] for k_ in ("qk", "bqk", "st", "bst", "kbt", "bkb", "lm", "blm"))
        psO, bO = kb.bank()
        P.op("pe", lambda e: e.matmul(psO[0:64, 0:256], lhsT=st[:, :], rhs=vt_[:, c, :], start=True, stop=False),
             reads=[bst, bv], writes=[bO], inc=False)
        for cc in range(2):
            P.op("pe", lambda e, cc=cc: e.matmul(
                psO[0:64, 0:256], lhsT=qk[:, cc, :], rhs=Sb[:, cc, :], start=False, stop=(cc == 1)),
                reads=[bqk, bSb], writes=[bO], inc=(cc == 1))
        evac_copy(kb, "act", ot_[:, c, :], psO[0:64, 0:256], [bO], [bo])
        psD, bD = kb.bank()
        for cc in range(2):
            P.op("pe", lambda e, cc=cc: e.matmul(
                psD[:, cc * 256:(cc + 1) * 256], lhsT=kbt[:, cc * 128:(cc + 1) * 128], rhs=vt_[:, c, :],
                start=True, stop=True), reads=[bkb, bv], writes=[bD], inc=(cc == 1))
        tm, btm = tmp[n % 2]
        P.op("dve", lambda e: e.tensor_tensor(
            out=tm[:], in0=psD[:, :], in1=S[:, :, :].rearrange("p a d -> p (a d)"), op=ALU.add),
            reads=[bD, bS], writes=[btm])
        for cc in range(2):
            P.op("dve", lambda e, cc=cc: e.tensor_scalar(
                out=S[:, cc, :], in0=tm[:, cc * 256:(cc + 1) * 256], scalar1=lm[:, 2 * cc:2 * cc + 1], scalar2=None,
                op0=ALU.mult), reads=[btm, blm], writes=[bS])
            P.op("act", lambda e, cc=cc: e.activation(
                out=Sb[:, cc, :], in_=tm[:, cc * 256:(cc + 1) * 256], func=AF.Copy, scale=lm[:, 2 * cc:2 * cc + 1]),
                reads=[btm, blm], writes=[bSb])
        if c == G - 1:
            sl = slice(gi * 64 * G, (gi + 1) * 64 * G)
            P.dma("sp", lambda e: e.dma_start(
                out=o_o[sl, :].rearrange("(c p) d -> p c d", p=64), in_=ot_[:]), reads=[bo], final=True)
        cx[n] = None

    load_group(0)
    for i in range(nch + 2):
        if i < nch:
            stage1(i)
        if 0 <= i - 1 < nch:
            stage2(i - 1)
        if 0 <= i - 2 < nch:
            stage3(i - 2)
    return kb.finish()


def res_proj(kb, ws, aT, baT, T, w_ap, res_src, res_bufs, dst, dst_bufs, stf, rst):
    P = kb.P
    for nb in range(4):
        def evac(tt, ps, bps, nb=nb):
            r, br = rst.next()
            rd = [res_bufs[tt]] if res_bufs is not None else []
            P.dma("sp", lambda e: e.dma_start(out=r[:], in_=res_src[tt * 128:(tt + 1) * 128, nb * 512:(nb + 1) * 512]),
                  reads=rd, writes=[br])
            s_, bs_ = stf.next()
            P.op("dve", lambda e: e.tensor_tensor(out=s_[:], in0=ps[:, :], in1=r[:], op=ALU.add),
                 reads=[bps, br], writes=[bs_])
            P.dma("sp", lambda e: e.dma_start(out=dst[tt * 128:(tt + 1) * 128, nb * 512:(nb + 1) * 512], in_=s_[:]),
                  reads=[bs_], writes=[dst_bufs[tt]])
        proj_tok(kb, ws, aT, baT, T, w_ap, nb * 512, 512, evac)


def emit_mlp(kb, ws, src, src_bufs, gt, gb, w_up, w_dn, dst, dst_bufs, T, xn5, bxn5, big, stf, rst, rl):
    P = kb.P
    TB = min(512, T)
    nts = TB // 128
    hT = big[:, 0:64 * TB].rearrange("p (f t) -> p f t", f=64)
    bhT = [kb.buf("hT") for _ in range(64)]
    for t0 in range(0, T, TB):
        kb.rows.run(src, t0, nts, gt, gb, src_bufs=src_bufs, dstT=xn5, bdstT=bxn5, col0=0)
        for f4 in range(16):
            def evac(j, tt0, nt, ps, bps, f4=f4):
                r, br = rl.next()
                P.op("act", lambda e: e.activation(out=r[:, 0:nt], in_=ps[:, 0:nt], func=AF.Relu), [bps], [br])
                P.op("dve", lambda e: e.tensor_tensor(out=hT[:, f4 * 4 + j, 0:nt], in0=ps[:, 0:nt], in1=r[:, 0:nt],
                                                      op=ALU.mult), [bps, br], [bhT[f4 * 4 + j]])
            proj_feat(kb, ws, xn5, bxn5, TB, w_up, f4 * 512, 512, evac, tb=TB)
        for nb in range(4):
            banks = [kb.bank() for _ in range(nts)]
            for kq in range(4):
                wt, wb = ws.load(w_dn, kq * 2048, 2048, nb * 512, 512)
                for ts in range(nts):
                    ps, bps = banks[ts]
                    for kc in range(16):
                        f = kq * 16 + kc
                        P.op("pe", lambda e, ps=ps, f=f, ts=ts, kc=kc, wt=wt, kq=kq: e.matmul(
                            ps[:, :], lhsT=hT[:, f, ts * 128:(ts + 1) * 128], rhs=wt[:, kc, :],
                            start=(kq == 0 and kc == 0), stop=(kq == 3 and kc == 15)),
                            reads=[bhT[f], wb], writes=[bps], inc=(kc == 15))
            for ts in range(nts):
                ps, bps = banks[ts]
                r0 = t0 + ts * 128
                tt = r0 // 128
                r, br = rst.next()
                rd = [src_bufs[tt]] if src_bufs is not None else []
                P.dma("sp", lambda e, r=r, r0=r0, nb=nb: e.dma_start(
                    out=r[:], in_=src[r0:r0 + 128, nb * 512:(nb + 1) * 512]), reads=rd, writes=[br])
                s_, bs_ = stf.next()
                P.op("dve", lambda e, s_=s_, ps=ps, r=r: e.tensor_tensor(out=s_[:], in0=ps[:, :], in1=r[:], op=ALU.add),
                     reads=[bps, br], writes=[bs_])
                P.dma("sp", lambda e, s_=s_, r0=r0, nb=nb: e.dma_start(
                    out=dst[r0:r0 + 128, nb * 512:(nb + 1) * 512], in_=s_[:]), reads=[bs_], writes=[dst_bufs[tt]])


def common_setup(kb, ident_d):
    kb.init_psum()
    load_consts(kb, ident_d)
    kb.eps, kb.beps = kb.sbb("eps", [128, 1], F32)
    kb.P.op("dve", lambda e: e.memset(kb.eps[:], EPS), writes=[kb.beps])
    kb.rows = Rows(kb)


class GVec:
    def __init__(self, kb):
        self.kb = kb
        self.t = [kb.sbb("gvec%d" % i, [128, D], F32) for i in range(2)]
        self.i = 0

    def load(self, vec_ap):
        t, b = self.t[self.i % 2]
        self.i += 1
        self.kb.P.dma("sp", lambda e: e.dma_start(out=t[:], in_=vec_ap.partition_broadcast(128)), writes=[b])
        return t, b


def build_C(T):
    kb = KB()
    P = kb.P
    o = kb.inp("o", [T, D])
    sr = kb.inp("sr", [T, D], BF16)
    x = kb.inp("x", [T, D])
    hg4 = kb.inp("hg4", [D])
    w_out = kb.inp("w_out", [D, D])
    g2 = kb.inp("g2", [D])
    w_up = kb.inp("w_up", [D, DFF])
    w_dn = kb.inp("w_dn", [DFF, D])
    gkv = kb.inp("gkv", [D])
    kv_w = kb.inp("kv_w", [D, 1024])
    g1b = kb.inp("g1b", [D])
    w_q = kb.inp("w_q", [D, D])
    ident_d = kb.inp("ident", [128, 128])
    h1, _ = kb.out("h1", [T, D])
    KT_o, _ = kb.out("KT", [4, 128, T], BF16)
    V_o, _ = kb.out("V", [T, 512], BF16)
    QT_o, _ = kb.out("QT", [16, 128, T], BF16)
    hA = kb.dram("hA", [T, D], F32)
    common_setup(kb, ident_d)
    ntt = T // 128
    bhA = [kb.buf("hA") for _ in range(ntt)]
    bh1 = [kb.buf("h1") for _ in range(ntt)]
    big = kb.sb("big", [128, 32768], BF16)
    big3 = big[:, 0:16 * T].rearrange("p (k t) -> p k t", k=16)
    bbig = [kb.buf("big") for _ in range(ntt)]
    TB = min(512, T)
    xn5 = kb.sb("xn5", [128, 16, TB], BF16)
    bxn5 = [kb.buf("xn5") for _ in range(TB // 128)]
    ws = WStream(kb)
    gv = GVec(kb)
    stf = Stager(kb, "stf", [128, 512], F32)
    stb = Stager(kb, "stb", [128, 512], BF16)
    rst = Stager(kb, "rst", [128, 512], F32)
    rl = Stager(kb, "rl", [128, 512], F32, n=2)
    gt, gb = gv.load(hg4)
    kb.rows.run(o, 0, ntt, gt, gb, nheads=4, mul_src=sr, dstT=big3, bdstT=bbig)
    res_proj(kb, ws, big3, bbig, T, w_out, x, None, hA, bhA, stf, rst)
    P.barrier()
    gt, gb = gv.load(g2)
    emit_mlp(kb, ws, hA, bhA, gt, gb, w_up, w_dn, h1, bh1, T, xn5, bxn5, big, stf, rst, rl)
    P.barrier()
    gt, gb = gv.load(gkv)
    kb.rows.run(h1, 0, ntt, gt, gb, src_bufs=bh1, dstT=big3, bdstT=bbig)

    def evacK(j, t0, nt, ps, bps):
        s_, bs_ = stb.next()
        evac_copy(kb, kb.evac_eng(), s_[:, 0:nt], ps[:, 0:nt], [bps], [bs_])
        P.dma("sp", lambda e: e.dma_start(out=KT_o[j, :, t0:t0 + nt], in_=s_[:, 0:nt]), reads=[bs_], final=True)
    proj_feat(kb, ws, big3, bbig, T, kv_w, 0, 512, evacK)

    def evacV(tt, ps, bps):
        s_, bs_ = stb.next()
        evac_copy(kb, kb.evac_eng(), s_[:], ps[:, :], [bps], [bs_])
        P.dma("sp", lambda e: e.dma_start(out=V_o[tt * 128:(tt + 1) * 128, :], in_=s_[:]), reads=[bs_], final=True)
    proj_tok(kb, ws, big3, bbig, T, kv_w, 512, 512, evacV)
    gt, gb = gv.load(g1b)
    kb.rows.run(h1, 0, ntt, gt, gb, src_bufs=bh1, dstT=big3, bdstT=bbig)
    for nb in range(4):
        def evacQ(j, t0, nt, ps, bps, nb=nb):
            s_, bs_ = stb.next()
            evac_copy(kb, kb.evac_eng(), s_[:, 0:nt], ps[:, 0:nt], [bps], [bs_], scale=SB_DH ** -0.5)
            P.dma("sp", lambda e: e.dma_start(out=QT_o[nb * 4 + j, :, t0:t0 + nt], in_=s_[:, 0:nt]), reads=[bs_], final=True)
        proj_feat(kb, ws, big3, bbig, T, w_q, nb * 512, 512, evacQ)
    for b in bh1:
        for k_, v_ in b.w.items():
            if P.final.get(k_, 0) < v_:
                P.final[k_] = v_
    return kb.finish()


def build_E(T):
    kb = KB()
    P = kb.P
    OT = kb.inp("OT", [16, 128, T], BF16)
    h1 = kb.inp("h1", [T, D])
    w_out = kb.inp("w_out", [D, D])
    g2 = kb.inp("g2", [D])
    w_up = kb.inp("w_up", [D, DFF])
    w_dn = kb.inp("w_dn", [DFF, D])
    gf = kb.inp("gf", [D])
    ident_d = kb.inp("ident", [128, 128])
    y, _ = kb.out("y", [T, D])
    hB = kb.dram("hB", [T, D], F32)
    h2 = kb.dram("h2", [T, D], F32)
    common_setup(kb, ident_d)
    ntt = T // 128
    bhB = [kb.buf("hB") for _ in range(ntt)]
    bh2 = [kb.buf("h2") for _ in range(ntt)]
    big = kb.sb("big", [128, 32768], BF16)
    big3 = big[:, 0:16 * T].rearrange("p (k t) -> p k t", k=16)
    bbig = [kb.buf("big") for _ in range(ntt)]
    TB = min(512, T)
    xn5 = kb.sb("xn5", [128, 16, TB], BF16)
    bxn5 = [kb.buf("xn5") for _ in range(TB // 128)]
    ws = WStream(kb)
    gv = GVec(kb)
    stf = Stager(kb, "stf", [128, 512], F32)
    rst = Stager(kb, "rst", [128, 512], F32)
    rl = Stager(kb, "rl", [128, 512], F32, n=2)
    for h in range(16):
        P.dma("sp", lambda e, h=h: e.dma_start(out=big3[:, h, :], in_=OT[h, :, :]), writes=bbig, key=("dma", "OTload"))
    res_proj(kb, ws, big3, bbig, T, w_out, h1, None, hB, bhB, stf, rst)
    P.barrier()
    gt, gb = gv.load(g2)
    emit_mlp(kb, ws, hB, bhB, gt, gb, w_up, w_dn, h2, bh2, T, xn5, bxn5, big, stf, rst, rl)
    gt, gb = gv.load(gf)
    kb.rows.run(h2, 0, ntt, gt, gb, src_bufs=bh2, dst_dram=y, final=True)
    return kb.finish()


def build_D(N):
    kb = KB()
    P = kb.P
    QT2 = kb.inp("QT2", [2, 128, N], BF16)
    KT_d = kb.inp("KT", [128, N], BF16)
    V_d = kb.inp("V", [N, 128], BF16)
    masks_d = kb.inp("masks", [128, 4, 512])
    negU_d = kb.inp("negU", [128, 128], BF16)
    onesb_d = kb.inp("onesb", [128, 128], BF16)
    OT2, _ = kb.out("OT2", [2, 128, N], BF16)
    kb.init_psum()
    obanks = kb.banks[:2]
    kb.banks = kb.banks[2:]
    NB = N // 128
    NQ = N // 512
    KT, bKT = kb.sbb("KT", [128, N], BF16)
    Vt, bVt = kb.sbb("Vt", [128, NB, 128], BF16)
    QT = [kb.sbb("QT%d" % i, [128, N], BF16) for i in range(2)]
    mk, bmk = kb.sbb("mk", [128, 4, 512], F32)
    negU, bnegU = kb.sbb("negU", [128, 128], BF16)
    onesb, bonesb = kb.sbb("onesb", [128, 128], BF16)
    P.dma("sp", lambda e: e.dma_start(out=KT[:], in_=KT_d[:, :]), writes=[bKT])
    P.dma("sp", lambda e: e.dma_start(out=Vt[:], in_=V_d[:, :].rearrange("(b p) d -> p b d", p=128)), writes=[bVt])
    for i in range(2):
        P.dma("sp", lambda e, i=i: e.dma_start(out=QT[i][0][:], in_=QT2[i, :, :]), writes=[QT[i][1]])
    P.dma("sp", lambda e: e.dma_start(out=mk[:], in_=masks_d[:, :, :]), writes=[bmk])
    P.dma("sp", lambda e: e.dma_start(out=negU[:], in_=negU_d[:, :]), writes=[bnegU])
    P.dma("sp", lambda e: e.dma_start(out=onesb[:], in_=onesb_d[:, :]), writes=[bonesb])
    et = Stager(kb, "e", [128, 512], F32, n=3)
    spf = Stager(kb, "spf", [128, 512], F32, n=2)
    spb = Stager(kb, "spb", [128, 512], BF16, n=3)
    lwt = Stager(kb, "lw", [128, 512], F32, n=3)
    wf = Stager(kb, "wf", [128, 512], F32, n=2)
    wb_ = Stager(kb, "wb", [128, 512], BF16, n=3)
    ost = Stager(kb, "ost", [128, 512], BF16, n=2)
    R, bR = kb.sbb("R", [128, 512], F32)
    banksS = kb.banks[0:2]
    banksL = kb.banks[2:4]
    banksT = kb.banks[4:6]
    steps = []
    for h in range(2):
        for qt in range(NQ):
            for kb_ in range(4 * qt + 3, -1, -1):
                steps.append((h, qt, kb_, kb_ == 4 * qt + 3))
    nS = len(steps)
    ctx = [None] * nS
    tile_bank = {}

    def stageA_pe(n):
        h, qt, kb_, first = steps[n]
        Qh, bQh = QT[h]
        qs = slice(qt * 512, (qt + 1) * 512)
        ks = slice(kb_ * 128, (kb_ + 1) * 128)
        psS, bS_ = banksS[n % 2]
        P.op("pe", lambda e: e.matmul(psS[:, :], lhsT=KT[:, ks], rhs=Qh[:, qs], start=True, stop=True),
             reads=[bKT, bQh], writes=[bS_])
        ctx[n] = dict(Qh=Qh, bQh=bQh, qs=qs, ks=ks, d=kb_ - 4 * qt)

    def stageA(n):
        c = ctx[n]
        d = c["d"]
        psS, bS_ = banksS[n % 2]
        e_, be = et.next()
        P.op("act", lambda e: e.activation(out=e_[:], in_=psS[:, :], func=AF.Exp), reads=[bS_], writes=[be])
        sb_, bsb = spb.next()
        if d >= 0:
            sf, bsf = spf.next()
            P.op("act", lambda e: e.activation(out=sf[:], in_=e_[:], func=AF.Ln, bias=1.0), reads=[be], writes=[bsf])
            P.op("dve", lambda e: e.tensor_tensor(out=sb_[:], in0=sf[:], in1=mk[:, d, :], op=ALU.mult),
                 reads=[bsf, bmk], writes=[bsb])
        else:
            P.op("act", lambda e: e.activation(out=sb_[:], in_=e_[:], func=AF.Ln, bias=1.0), reads=[be], writes=[bsb])
        c["sb_"], c["bsb"] = sb_, bsb

    def stageB(n):
        h, qt, kb_, first = steps[n]
        c = ctx[n]
        sb_, bsb, Qh, bQh, qs, ks = c["sb_"], c["bsb"], c["Qh"], c["bQh"], c["qs"], c["ks"]
        psL, bL = banksL[n % 2]
        psT, bT = banksT[n % 2]
        P.op("pe", lambda e: e.matmul(psL[:, :], lhsT=KT[:, ks], rhs=Qh[:, qs], start=True, stop=False),
             reads=[bKT, bQh], writes=[bL], inc=False)
        P.op("pe", lambda e: e.matmul(psL[:, :], lhsT=negU[:, :], rhs=sb_[:], start=False, stop=True),
             reads=[bnegU, bsb], writes=[bL])
        if kb_ > 0:
            P.op("pe", lambda e: e.matmul(psT[:, :], lhsT=onesb[:, :], rhs=sb_[:], start=True, stop=True),
                 reads=[bonesb, bsb], writes=[bT])
        lw, blw = lwt.next()
        if first:
            P.op("dve", lambda e: e.tensor_copy(out=lw[:], in_=psL[:, :]), reads=[bL], writes=[blw])
            P.op("dve", lambda e: e.tensor_copy(out=R[:], in_=psT[:, :]), reads=[bT], writes=[bR])
        else:
            P.op("dve", lambda e: e.tensor_tensor(out=lw[:], in0=psL[:, :], in1=R[:], op=ALU.subtract),
                 reads=[bL, bR], writes=[blw])
            if kb_ > 0:
                P.op("dve", lambda e: e.tensor_tensor(out=R[:], in0=psT[:, :], in1=R[:], op=ALU.add),
                     reads=[bT, bR], writes=[bR])
        c["lw"], c["blw"] = lw, blw

    def stageC(n):
        h, qt, kb_, first = steps[n]
        c = ctx[n]
        lw, blw, d, qs = c["lw"], c["blw"], c["d"], c["qs"]
        if first:
            tile_bank[(h, qt)] = obanks[len(tile_bank) % 2]
        psO, bO = tile_bank[(h, qt)]
        w_, bw = wb_.next()
        if d >= 0:
            wf_, bwf = wf.next()
            P.op("act", lambda e: e.activation(out=wf_[:], in_=lw[:], func=AF.Exp), reads=[blw], writes=[bwf])
            P.op("dve", lambda e: e.tensor_tensor(out=w_[:], in0=wf_[:], in1=mk[:, d, :], op=ALU.mult),
                 reads=[bwf, bmk], writes=[bw])
        else:
            P.op("act", lambda e: e.activation(out=w_[:], in_=lw[:], func=AF.Exp), reads=[blw], writes=[bw])
        P.op("pe", lambda e: e.matmul(psO[:, :], lhsT=Vt[:, kb_, :], rhs=w_[:], start=first, stop=(kb_ == 0)),
             reads=[bVt, bw], writes=[bO])
        if kb_ == 0:
            os_, bos = ost.next()
            evac_copy(kb, "dve", os_[:], psO[:, :], [bO], [bos])
            P.dma("sp", lambda e: e.dma_start(out=OT2[h, :, qs], in_=os_[:]), reads=[bos], final=True)
        ctx[n] = None

    stageA_pe(0)
    for i in range(nS + 2):
        if i + 1 < nS:
            stageA_pe(i + 1)
        if i < nS:
            stageA(i)
        if 0 <= i - 1 < nS:
            stageB(i - 1)
        if 0 <= i - 2 < nS:
            stageC(i - 2)
    return kb.finish()


_CACHE = {}


def _get(name, fn, *a):
    key = (name,) + a
    if key not in _CACHE:
        _CACHE[key] = fn(*a)
    return _CACHE[key]


def _run(nc, in_maps):
    res = run_bass_kernel_spmd(nc, in_maps, core_ids=list(range(NCORES)))
    return res.results


def attn_consts():
    import ml_dtypes
    j = np.arange(128)[:, None, None]
    d = np.arange(4)[None, :, None]
    i = np.arange(512)[None, None, :]
    masks = (j + 128 * d < i).astype(np.float32)
    negU = (-(np.arange(128)[:, None] >= np.arange(128)[None, :]).astype(np.float32)).astype(ml_dtypes.bfloat16)
    onesb = np.ones((128, 128), np.float32).astype(ml_dtypes.bfloat16)
    return masks, negU, onesb


def kernel(x, norm1_g, norm2_g, gla_w_in, gla_w_gate_up, gla_b_gate, gla_head_g, gla_w_out, kv_norm_g, kv_w,
           sb_w_q, sb_w_out, mlp_w_up, mlp_w_down, final_g):
    import ml_dtypes
    f = lambda a: np.ascontiguousarray(np.asarray(a, dtype=np.float32))
    x2 = f(x)[0]
    S = x2.shape[0]
    T = S // NCORES
    ident = np.eye(128, dtype=np.float32)
    cs = [slice(c * T, (c + 1) * T) for c in range(NCORES)]
    w_in = f(gla_w_in[0]); wgu = f(gla_w_gate_up[0]); bg = f(gla_b_gate[0]); g1 = f(norm1_g[0])
    rA = _run(_get("A", build_A, T), [{"x": f(x2[cs[c]]), "g1": g1, "w_in": w_in, "wgu": wgu, "bg": bg, "ident": ident}
                                      for c in range(NCORES)])
    cat = lambda n: np.concatenate([np.asarray(rA[c][n]) for c in range(NCORES)], 0)
    qf, kf, gf_, vf, srf = cat("q"), cat("k"), cat("g"), cat("v"), cat("sr")
    tri = np.triu(np.ones((64, 64), np.float32))
    identb = np.eye(64, dtype=np.float32).astype(ml_dtypes.bfloat16)
    ones = np.ones((64, 2), np.float32)
    mapsB = []
    for c in range(NCORES):
        h, d = c // 2, c % 2
        hs = slice(h * 256, (h + 1) * 256)
        mapsB.append({"q": f(qf[:, hs]), "k": f(kf[:, hs]), "g": f(gf_[:, hs]),
                      "v": np.ascontiguousarray(vf[:, h * 512 + d * 256: h * 512 + (d + 1) * 256]),
                      "tri": tri, "identb": identb, "ones": ones})
    rB = _run(_get("B", build_B, S), mapsB)
    of = np.empty((S, D), np.float32)
    for c in range(NCORES):
        h, d = c // 2, c % 2
        of[:, h * 512 + d * 256: h * 512 + (d + 1) * 256] = np.asarray(rB[c]["o"])
    hg4 = np.ascontiguousarray(np.tile(f(gla_head_g[0]), 4))
    cst = {"hg4": hg4, "w_out": f(gla_w_out[0]), "g2": f(norm2_g[0]), "w_up": f(mlp_w_up[0]), "w_dn": f(mlp_w_down[0]),
           "gkv": f(kv_norm_g), "kv_w": f(kv_w), "g1b": f(norm1_g[1]), "w_q": f(sb_w_q[0]), "ident": ident}
    rC = _run(_get("C", build_C, T), [dict(cst, o=f(of[cs[c]]), sr=np.ascontiguousarray(srf[cs[c]]), x=f(x2[cs[c]]))
                                      for c in range(NCORES)])
    h1 = [np.asarray(rC[c]["h1"]) for c in range(NCORES)]
    KTf = np.concatenate([np.asarray(rC[c]["KT"]) for c in range(NCORES)], 2)
    Vf = np.concatenate([np.asarray(rC[c]["V"]) for c in range(NCORES)], 0)
    QTf = np.concatenate([np.asarray(rC[c]["QT"]) for c in range(NCORES)], 2)
    masks, negU, onesb = attn_consts()
    mapsD = []
    for c in range(NCORES):
        kvh = c // 2
        mapsD.append({"QT2": np.ascontiguousarray(QTf[2 * c:2 * c + 2]), "KT": np.ascontiguousarray(KTf[kvh]),
                      "V": np.ascontiguousarray(Vf[:, kvh * 128:(kvh + 1) * 128]),
                      "masks": masks, "negU": negU, "onesb": onesb})
    rD = _run(_get("D", build_D, S), mapsD)
    OTf = np.concatenate([np.asarray(rD[c]["OT2"]) for c in range(NCORES)], 0)
    cst = {"w_out": f(sb_w_out[0]), "g2": f(norm2_g[1]), "w_up": f(mlp_w_up[1]), "w_dn": f(mlp_w_down[1]),
           "gf": f(final_g), "ident": ident}
    rE = _run(_get("E", build_E, T), [dict(cst, OT=np.ascontiguousarray(OTf[:, :, cs[c]]), h1=h1[c]) for c in range(NCORES)])
    y = np.concatenate([np.asarray(rE[c]["y"]) for c in range(NCORES)], 0)
    return y.reshape(1, S, D).astype(np.float32)
```

```python
import numpy as np
from contextlib import ExitStack
import concourse.bass as bass
import concourse.mybir as mybir
from concourse.bass_utils import run_bass_kernel_spmd

F32 = mybir.dt.float32
BF16 = mybir.dt.bfloat16
AF = mybir.ActivationFunctionType
ALU = mybir.AluOpType

NCORES = 8
D = 2048
SEQ = 16384
DFF = 8192
EPS = 1e-6
GLA_H, GLA_DK, GLA_DV = 4, 256, 512
GLA_IN = 6160
SB_H, SB_DH, SB_KVH = 16, 128, 4
ENGS = ("pe", "act", "dve", "pool", "sp")


class Buf:
    __slots__ = ("name", "w", "r")

    def __init__(self, name):
        self.name = name
        self.w = {}
        self.r = {}


class Prog:
    def __init__(self, nc):
        self.nc = nc
        self.q = {e: [] for e in ENGS}
        self.cnt = {}
        self.sems = {}
        self.seen = {e: {} for e in ENGS}
        self._stack = []
        self.final = {}

    def _sem(self, key):
        if key not in self.sems:
            cm = self.nc.semaphore("s%d" % len(self.sems))
            self.sems[key] = cm.__enter__()
            self._stack.append(cm)
            self.cnt[key] = 0
        return self.sems[key]

    def close(self):
        for cm in reversed(self._stack):
            cm.__exit__(None, None, None)

    def _deps(self, eng, reads, writes):
        need = {}
        for b in reads:
            for k, v in b.w.items():
                if need.get(k, 0) < v:
                    need[k] = v
        for b in writes:
            for d in (b.w, b.r):
                for k, v in d.items():
                    if need.get(k, 0) < v:
                        need[k] = v
        out = []
        seen = self.seen[eng]
        for k, v in need.items():
            if k == eng and eng == "pe":
                continue
            if seen.get(k, 0) >= v:
                continue
            seen[k] = v
            out.append((k, v))
        return out

    def _record(self, ev, reads, writes):
        k, v = ev
        for b in reads:
            if b.r.get(k, 0) < v:
                b.r[k] = v
        for b in writes:
            if b.w.get(k, 0) < v:
                b.w[k] = v

    def op(self, eng, fn, reads=(), writes=(), inc=True):
        waits = self._deps(eng, reads, writes)
        sem = self._sem(eng)
        if inc:
            self.cnt[eng] += 1
        ev = (eng, self.cnt[eng] if inc else self.cnt[eng] + 1)
        self._record(ev, reads, writes)
        sems = self.sems

        def run(e):
            for k, v in waits:
                e.wait_ge(sems[k], v)
            ins = fn(e)
            if inc:
                ins.then_inc(sem, 1)
        self.q[eng].append(run)

    def dma(self, eng, fn, reads=(), writes=(), key=None, final=False):
        waits = self._deps(eng, reads, writes)
        if key is None:
            key = ("dma", (writes[0] if writes else reads[0]).name)
        sem = self._sem(key)
        self.cnt[key] += 16
        ev = (key, self.cnt[key])
        self._record(ev, reads, writes)
        if final:
            self.final[key] = self.cnt[key]
        sems = self.sems

        def run(e):
            for k, v in waits:
                e.wait_ge(sems[k], v)
            fn(e).then_inc(sem, 16)
        self.q[eng].append(run)

    def coll(self, fn, reads=(), writes=(), key=None):
        eng = "pool"
        waits = self._deps(eng, reads, writes)
        sem = self._sem(key)
        self.cnt[key] += 1
        ev = (key, self.cnt[key])
        self._record(ev, reads, writes)
        sems = self.sems

        def run(e):
            for k, v in waits:
                e.wait_ge(sems[k], v)
            fn(e).then_inc(sem)
        self.q[eng].append(run)

    def barrier(self):
        snap = dict(self.cnt)
        sems = self.sems
        for eng in ENGS:
            waits = []
            seen = self.seen[eng]
            for k, v in snap.items():
                if v == 0 or k == eng or seen.get(k, 0) >= v:
                    continue
                seen[k] = v
                waits.append((k, v))

            def run(e, waits=waits):
                for k, v in waits:
                    e.wait_ge(sems[k], v)
            self.q[eng].append(run)

    def final_wait(self, eng):
        waits = list(self.final.items())
        sems = self.sems

        def run(e):
            for k, v in waits:
                e.wait_ge(sems[k], v)
        self.q[eng].append(run)

    def emit(self):
        nc = self.nc
        q = self.q
        with nc.Block() as block:
            @block.tensor
            def _(e):
                for f in q["pe"]:
                    f(e)

            @block.scalar
            def _(e):
                for f in q["act"]:
                    f(e)

            @block.vector
            def _(e):
                for f in q["dve"]:
                    f(e)

            @block.gpsimd
            def _(e):
                for f in q["pool"]:
                    f(e)

            @block.sync
            def _(e):
                for f in q["sp"]:
                    f(e)


class KB:
    def __init__(self):
        self.nc = bass.Bass("TRN2", target_bir_lowering=False)
        self.P = Prog(self.nc)
        self.es = ExitStack()
        self.nbuf = 0
        self.banks = []
        self.bi = 0
        self.outs = []
        self.rr = 0

    def dram(self, name, shape, dt, kind="Internal"):
        return self.nc.dram_tensor(name, list(shape), dt, kind=kind).ap()

    def inp(self, name, shape, dt=F32):
        return self.dram(name, shape, dt, "ExternalInput")

    def out(self, name, shape, dt=F32):
        ap = self.dram(name, shape, dt, "ExternalOutput")
        return ap, None

    def sb(self, name, shape, dt):
        return self.es.enter_context(self.nc.sbuf_tensor("sb_" + name, list(shape), dt))

    def buf(self, name="b"):
        self.nbuf += 1
        return Buf("%s_%d" % (name, self.nbuf))

    def sbb(self, name, shape, dt):
        return self.sb(name, shape, dt), self.buf(name)

    def init_psum(self, n=8):
        for i in range(n):
            t = self.es.enter_context(self.nc.psum_tensor("ps%d" % i, [128, 512], F32))
            self.banks.append((t, self.buf("ps%d" % i)))

    def bank(self):
        r = self.banks[self.bi % len(self.banks)]
        self.bi += 1
        return r

    def evac_eng(self):
        self.rr += 1
        return "act" if self.rr % 2 else "dve"

    def finish(self):
        self.P.final_wait("sp")
        self.P.emit()
        self.es.close()
        self.P.close()
        return self.nc


def evac_copy(kb, eng, out_ap, in_ap, reads, writes, scale=None):
    P = kb.P
    if eng == "act":
        if scale is None:
            P.op("act", lambda e: e.activation(out=out_ap, in_=in_ap, func=AF.Copy), reads, writes)
        else:
            P.op("act", lambda e: e.activation(out=out_ap, in_=in_ap, func=AF.Copy, scale=float(scale)), reads, writes)
    else:
        if scale is None:
            P.op(eng, lambda e: e.tensor_copy(out=out_ap, in_=in_ap), reads, writes)
        else:
            P.op(eng, lambda e: e.tensor_scalar(out=out_ap, in0=in_ap, scalar1=float(scale), scalar2=None,
                                                 op0=ALU.mult), reads, writes)


def load_consts(kb, ident_d):
    ident, bident = kb.sbb("ident", [128, 128], F32)
    kb.P.dma("sp", lambda e: e.dma_start(out=ident[:], in_=ident_d[:, :]), writes=[bident])
    kb.ident, kb.bident = ident, bident


def load_bcast(kb, name, vec_ap, n):
    t, b = kb.sbb(name, [128, n], F32)
    kb.P.dma("sp", lambda e: e.dma_start(out=t[:], in_=vec_ap.partition_broadcast(128)), writes=[b])
    return t, b


class Rows:
    def __init__(self, kb):
        self.kb = kb
        self.xin = [kb.sbb("r_xin%d" % i, [128, D], F32) for i in range(2)]
        self.xn = [kb.sbb("r_xn%d" % i, [128, D], F32) for i in range(2)]
        self.st = [kb.sbb("r_st%d" % i, [128, 12], F32) for i in range(2)]
        self.mul = None
        self.i = 0

    def run(self, src, row0, ntiles, gt, gb, src_bufs=None, nheads=1, mul_src=None,
            dstT=None, bdstT=None, col0=0, dst_dram=None, dst_bufs=None, final=False):
        kb = self.kb
        P = kb.P
        W = D // nheads
        if mul_src is not None and self.mul is None:
            self.mul = [kb.sbb("r_mul%d" % i, [128, D], BF16) for i in range(2)]
        for ti in range(ntiles):
            r0 = row0 + ti * 128
            tt = r0 // 128
            i2 = self.i % 2
            self.i += 1
            xt, bx = self.xin[i2]
            xnt, bxn = self.xn[i2]
            s, bs = self.st[i2]
            rd = [src_bufs[tt]] if src_bufs is not None else []
            P.dma("sp", lambda e, xt=xt, r0=r0: e.dma_start(out=xt[:], in_=src[r0:r0 + 128, :]), reads=rd, writes=[bx])
            if mul_src is not None:
                mt, bm = self.mul[i2]
                P.dma("sp", lambda e, mt=mt, r0=r0: e.dma_start(out=mt[:], in_=mul_src[r0:r0 + 128, :]), writes=[bm])
            for h in range(nheads):
                P.op("act", lambda e, xt=xt, xnt=xnt, s=s, h=h: e.activation(
                    out=xnt[:, h * W:(h + 1) * W], in_=xt[:, h * W:(h + 1) * W], func=AF.Square, accum_out=s[:, h:h + 1]),
                    reads=[bx], writes=[bxn, bs])
            P.op("act", lambda e, s=s: e.activation(out=s[:, 4:4 + nheads], in_=s[:, 0:nheads], func=AF.Sqrt, scale=1.0 / W,
                                                    bias=kb.eps[:, 0:1]), reads=[bs, kb.beps], writes=[bs])
            P.op("dve", lambda e, s=s: e.reciprocal(out=s[:, 8:8 + nheads], in_=s[:, 4:4 + nheads]), reads=[bs], writes=[bs])
            for h in range(nheads):
                P.op("dve", lambda e, xt=xt, xnt=xnt, s=s, h=h: e.scalar_tensor_tensor(
                    out=xnt[:, h * W:(h + 1) * W], in0=xt[:, h * W:(h + 1) * W], scalar=s[:, 8 + h:9 + h],
                    in1=gt[:, h * W:(h + 1) * W], op0=ALU.mult, op1=ALU.mult),
                    reads=[bx, bs, gb], writes=[bxn])
            if mul_src is not None:
                P.op("dve", lambda e, xnt=xnt, mt=mt: e.tensor_tensor(out=xnt[:], in0=xnt[:], in1=mt[:], op=ALU.mult),
                     reads=[bxn, bm], writes=[bxn])
            if dstT is not None:
                c = col0 + ti * 128
                for b4 in range(4):
                    ps, bps = kb.bank()
                    for cc in range(4):
                        kc = b4 * 4 + cc
                        P.op("pe", lambda e, ps=ps, xnt=xnt, kc=kc, cc=cc: e.transpose(
                            ps[:, cc * 128:(cc + 1) * 128], xnt[:, kc * 128:(kc + 1) * 128], kb.ident[:]),
                            reads=[bxn, kb.bident], writes=[bps], inc=(cc == 3))
                    evac_copy(kb, kb.evac_eng(), dstT[:, b4 * 4:(b4 + 1) * 4, c:c + 128],
                              ps[:, :].rearrange("p (c t) -> p c t", c=4), [bps], [bdstT[c // 128]])
            if dst_dram is not None:
                wr = [dst_bufs[tt]] if dst_bufs is not None else []
                P.dma("sp", lambda e, xnt=xnt, r0=r0: e.dma_start(out=dst_dram[r0:r0 + 128, :], in_=xnt[:]),
                      reads=[bxn], writes=wr, final=final)


def emit_norm_T(kb, src, T, gt, gb, xnT, bxnT, nm):
    if not hasattr(kb, "rows"):
        kb.rows = Rows(kb)
    kb.rows.run(src, 0, T // 128, gt, gb, dstT=xnT, bdstT=bxnT)


class WStream:
    def __init__(self, kb, nbuf=2):
        self.kb = kb
        self.t = [kb.sbb("wblk%d" % i, [128, 16, 512], BF16) for i in range(nbuf)]
        self.i = 0

    def load(self, w_ap, r0, nrows, c0, ncols):
        kb = self.kb
        t, b = self.t[self.i % len(self.t)]
        self.i += 1
        nk = nrows // 128
        src = w_ap[r0:r0 + nrows, c0:c0 + ncols].rearrange("(kc p) n -> p kc n", p=128)
        kb.P.dma("pool", lambda e: e.dma_start(out=t[:, 0:nk, 0:ncols], in_=src), writes=[b])
        return t, b


def proj_tok(kb, ws, xnT, bxnT, T, w_ap, c0, ncols, evac, nk=16, r0=0):
    P = kb.P
    wt, wb = ws.load(w_ap, r0, nk * 128, c0, ncols)
    for tt in range(T // 128):
        ps, bps = kb.bank()
        for kc in range(nk):
            P.op("pe", lambda e, ps=ps, kc=kc, tt=tt: e.matmul(
                ps[:, 0:ncols], lhsT=xnT[:, kc, tt * 128:(tt + 1) * 128], rhs=wt[:, kc, 0:ncols],
                start=(kc == 0), stop=(kc == nk - 1)),
                reads=[bxnT[tt], wb], writes=[bps], inc=(kc == nk - 1))
        evac(tt, ps, bps)


def proj_feat(kb, ws, xnT, bxnT, T, w_ap, c0, ncols, evac, nk=16, tb=512, tok0=0):
    P = kb.P
    wt, wb = ws.load(w_ap, 0, nk * 128, c0, ncols)
    for t0 in range(0, T, tb):
        nt = min(tb, T - t0)
        rd = [bxnT[(tok0 + t0) // 128 + i] for i in range(nt // 128)]
        for j in range((ncols + 127) // 128):
            m = min(128, ncols - j * 128)
            ps, bps = kb.bank()
            for kc in range(nk):
                P.op("pe", lambda e, ps=ps, kc=kc, j=j, m=m, t0=t0, nt=nt: e.matmul(
                    ps[0:m, 0:nt], lhsT=wt[:, kc, j * 128:j * 128 + m], rhs=xnT[:, kc, tok0 + t0:tok0 + t0 + nt],
                    start=(kc == 0), stop=(kc == nk - 1)),
                    reads=rd + [wb], writes=[bps], inc=(kc == nk - 1))
            evac(j, t0, nt, ps, bps)


class Stager:
    def __init__(self, kb, name, shape, dt, n=3):
        self.kb = kb
        self.t = [kb.sbb(name + str(i), shape, dt) for i in range(n)]
        self.i = 0

    def next(self):
        r = self.t[self.i % len(self.t)]
        self.i += 1
        return r


def build_A(T):
    kb = KB()
    P = kb.P
    x = kb.inp("x", [T, D])
    g1 = kb.inp("g1", [D])
    w_in = kb.inp("w_in", [D, GLA_IN])
    wgu = kb.inp("wgu", [16, 1024])
    bg = kb.inp("bg", [1024])
    ident_d = kb.inp("ident", [128, 128])
    q_o, bq_o = kb.out("q", [T, 1024])
    k_o, bk_o = kb.out("k", [T, 1024])
    g_o, bg_o = kb.out("g", [T, 1024])
    v_o, bv_o = kb.out("v", [T, 2048], BF16)
    sr_o, bsr_o = kb.out("sr", [T, 2048], BF16)
    kb.init_psum()
    load_consts(kb, ident_d)
    kb.eps, kb.beps = kb.sbb("eps", [128, 1], F32)
    P.op("dve", lambda e: e.memset(kb.eps[:], EPS), writes=[kb.beps])
    gt, gb = load_bcast(kb, "g1b", g1, D)
    bgt, bgb = load_bcast(kb, "bgb", bg, 1024)
    wgu_t, wgu_b = kb.sbb("wgu", [16, 1024], F32)
    P.dma("sp", lambda e: e.dma_start(out=wgu_t[:], in_=wgu[:, :]), writes=[wgu_b])
    xnT = kb.sb("xnT", [128, 16, T], BF16)
    bxnT = [kb.buf("xnT") for _ in range(T // 128)]
    emit_norm_T(kb, x, T, gt, gb, xnT, bxnT, "n1")
    ws = WStream(kb)
    stf = Stager(kb, "stf", [128, 512], F32)
    stb = Stager(kb, "stb", [128, 512], BF16)

    def mk_evac(dst, bdst, col, kind, scale=None):
        def evac(tt, ps, bps):
            if kind == "f32":
                s, bs = stf.next()
                evac_copy(kb, kb.evac_eng(), s[:], ps[:, :], [bps], [bs], scale=scale)
            elif kind == "bf16":
                s, bs = stb.next()
                evac_copy(kb, kb.evac_eng(), s[:], ps[:, :], [bps], [bs])
            else:
                s, bs = stb.next()
                P.op("act", lambda e: e.activation(out=s[:], in_=ps[:, :], func=AF.Silu), [bps], [bs])
            P.dma("sp", lambda e: e.dma_start(out=dst[tt * 128:(tt + 1) * 128, col:col + 512], in_=s[:]),
                  reads=[bs], final=True)
        return evac

    for nb in range(2):
        proj_tok(kb, ws, xnT, bxnT, T, w_in, nb * 512, 512, mk_evac(q_o, bq_o, nb * 512, "f32", GLA_DK ** -0.5))
    for nb in range(2):
        proj_tok(kb, ws, xnT, bxnT, T, w_in, 1024 + nb * 512, 512, mk_evac(k_o, bk_o, nb * 512, "f32"))
    for nb in range(4):
        proj_tok(kb, ws, xnT, bxnT, T, w_in, 2048 + nb * 512, 512, mk_evac(v_o, bv_o, nb * 512, "bf16"))
    alT, balT = kb.sbb("alT", [16, T], F32)
    def evac_al(j, t0, nt, ps, bps):
        evac_copy(kb, "dve", alT[:, t0:t0 + nt], ps[0:16, 0:nt], [bps], [balT])
    proj_feat(kb, ws, xnT, bxnT, T, w_in, 6144, 16, evac_al)
    ez = [kb.sbb("ez%d" % i, [128, 512], F32) for i in range(2)]
    for tt in range(T // 128):
        for hb in range(2):
            ps, bps = kb.bank()
            P.op("pe", lambda e, ps=ps, tt=tt, hb=hb: e.matmul(
                ps[:, :], lhsT=alT[:, tt * 128:(tt + 1) * 128], rhs=wgu_t[:, hb * 512:(hb + 1) * 512],
                start=True, stop=True), reads=[balT, wgu_b], writes=[bps])
            z, bz = ez[(tt * 2 + hb) % 2]
            P.op("dve", lambda e, z=z, ps=ps, hb=hb: e.tensor_tensor(
                out=z[:], in0=ps[:, :], in1=bgt[:, hb * 512:(hb + 1) * 512], op=ALU.add),
                reads=[bps, bgb], writes=[bz])
            P.op("act", lambda e, z=z: e.activation(out=z[:], in_=z[:], func=AF.Exp, scale=-1.0), [bz], [bz])
            s, bs = stf.next()
            P.op("act", lambda e, z=z, s=s: e.activation(out=s[:], in_=z[:], func=AF.Ln, bias=1.0), [bz], [bs])
            P.op("dve", lambda e, s=s: e.tensor_scalar(out=s[:], in0=s[:], scalar1=-1.0 / 16.0, scalar2=None,
                                                         op0=ALU.mult), [bs], [bs])
            P.dma("sp", lambda e, s=s, tt=tt, hb=hb: e.dma_start(
                out=g_o[tt * 128:(tt + 1) * 128, hb * 512:(hb + 1) * 512], in_=s[:]), reads=[bs], final=True)
    for nb in range(4):
        proj_tok(kb, ws, xnT, bxnT, T, w_in, 4096 + nb * 512, 512, mk_evac(sr_o, bsr_o, nb * 512, "silu"))
    return kb.finish()


def build_B(N):
    kb = KB()
    P = kb.P
    q = kb.inp("q", [N, 256])
    k = kb.inp("k", [N, 256])
    g = kb.inp("g", [N, 256])
    v = kb.inp("v", [N, 256], BF16)
    tri_d = kb.inp("tri", [64, 64])
    idb_d = kb.inp("identb", [64, 64], BF16)
    ones_d = kb.inp("ones", [64, 2])
    o_o, _ = kb.out("o", [N, 256])
    kb.init_psum()
    tri, btri = kb.sbb("tri", [64, 64], F32)
    idb, bidb = kb.sbb("idb", [64, 64], BF16)
    ones, bones = kb.sbb("ones", [64, 2], F32)
    P.dma("sp", lambda e: e.dma_start(out=tri[:], in_=tri_d[:, :]), writes=[btri])
    P.dma("sp", lambda e: e.dma_start(out=idb[:], in_=idb_d[:, :]), writes=[bidb])
    P.dma("sp", lambda e: e.dma_start(out=ones[:], in_=ones_d[:, :]), writes=[bones])
    S, bS = kb.sbb("S", [128, 2, 256], F32)
    Sb, bSb = kb.sbb("Sb", [128, 2, 256], BF16)
    P.op("dve", lambda e: e.memset(S[:], 0.0), writes=[bS])
    P.op("dve", lambda e: e.memset(Sb[:], 0.0), writes=[bSb])
    G = 8
    qg = [kb.sbb("qg%d" % i, [64, G, 256], F32) for i in range(2)]
    kg = [kb.sbb("kg%d" % i, [64, G, 256], F32) for i in range(2)]
    gg = [kb.sbb("gg%d" % i, [64, G, 256], F32) for i in range(2)]
    vg = [kb.sbb("vg%d" % i, [64, G, 256], BF16) for i in range(2)]
    og = [kb.sbb("og%d" % i, [64, G, 256], F32) for i in range(2)]
    NBUF = 3
    ecum = [kb.sbb("ecum%d" % i, [64, 256], F32) for i in range(NBUF)]
    encum = [kb.sbb("encum%d" % i, [64, 256], F32) for i in range(NBUF)]
    qtb = [kb.sbb("qtb%d" % i, [64, 256], BF16) for i in range(NBUF)]
    ktb = [kb.sbb("ktb%d" % i, [64, 256], BF16) for i in range(NBUF)]
    qkT = [kb.sbb("qkT%d" % i, [128, 4, 64], BF16) for i in range(NBUF)]
    sTb = [kb.sbb("sTb%d" % i, [64, 64], BF16) for i in range(NBUF)]
    lam = [kb.sbb("lam%d" % i, [128, 4], F32) for i in range(NBUF)]
    tmp = [kb.sbb("tmp%d" % i, [128, 512], F32) for i in range(2)]
    ngrp = N // (64 * G)
    nch = ngrp * G

    def load_group(gi):
        sl = slice(gi * 64 * G, (gi + 1) * 64 * G)
        for (tl, src) in ((qg, q), (kg, k), (gg, g), (vg, v)):
            t, b = tl[gi % 2]
            P.dma("sp", lambda e, t=t, src=src: e.dma_start(
                out=t[:], in_=src[sl, :].rearrange("(c p) d -> p c d", p=64)), writes=[b])

    def grp(n):
        gi, c = n // G, n % G
        return gi, c, qg[gi % 2], kg[gi % 2], gg[gi % 2], vg[gi % 2], og[gi % 2]

    cx = [None] * nch

    def stage1(n):
        gi, c, (qt_, bq), (kt_, bk), (gt_, bgg), (vt_, bv), (ot_, bo) = grp(n)
        if c == 2 and gi + 1 < ngrp:
            load_group(gi + 1)
        i3 = n % NBUF
        psA, bA = kb.bank()
        P.op("pe", lambda e: e.matmul(psA[0:64, 0:256], lhsT=tri[:, :], rhs=gt_[:, c, :], start=True, stop=True),
             reads=[btri, bgg], writes=[bA], inc=False)
        for cc in range(2):
            P.op("pe", lambda e, cc=cc: e.matmul(
                psA[:, 256 + 2 * cc:258 + 2 * cc], lhsT=gt_[:, c, cc * 128:(cc + 1) * 128], rhs=ones[:, 0:2],
                start=True, stop=True), reads=[bgg, bones], writes=[bA], inc=(cc == 1))
        lm, blm = lam[i3]
        P.op("act", lambda e: e.activation(out=lm[:, 0:4], in_=psA[:, 256:260], func=AF.Exp), reads=[bA], writes=[blm])
        ec, bec = ecum[i3]
        en, ben = encum[i3]
        P.op("act", lambda e: e.activation(out=ec[:], in_=psA[0:64, 0:256], func=AF.Exp), reads=[bA], writes=[bec])
        P.op("act", lambda e: e.activation(out=en[:], in_=psA[0:64, 0:256], func=AF.Exp, scale=-1.0), reads=[bA], writes=[ben])
        qb, bqb = qtb[i3]
        kbt, bkb = ktb[i3]
        P.op("dve", lambda e: e.tensor_tensor(out=qb[:], in0=qt_[:, c, :], in1=ec[:], op=ALU.mult),
             reads=[bq, bec], writes=[bqb])
        P.op("dve", lambda e: e.tensor_tensor(out=kbt[:], in0=kt_[:, c, :], in1=en[:], op=ALU.mult),
             reads=[bk, ben], writes=[bkb])
        cx[n] = dict(lm=lm, blm=blm, qb=qb, bqb=bqb, kbt=kbt, bkb=bkb)

    def stage2(n):
        x_ = cx[n]
        qb, bqb, kbt, bkb = x_["qb"], x_["bqb"], x_["kbt"], x_["bkb"]
        i3 = n % NBUF
        psT, bT = kb.bank()
        for idx, (src, bsrc) in enumerate(((qb, bqb), (qb, bqb), (kbt, bkb), (kbt, bkb))):
            cc = idx % 2
            P.op("pe", lambda e, src=src, cc=cc, idx=idx: e.matmul(
                psT[:, idx * 64:(idx + 1) * 64], lhsT=src[:, cc * 128:(cc + 1) * 128], rhs=idb[:, :],
                start=True, stop=True), reads=[bsrc, bidb], writes=[bT], inc=(idx == 3))
        qk, bqk = qkT[i3]
        evac_copy(kb, "act", qk[:, :, :], psT[:, 0:256].rearrange("p (a t) -> p a t", a=4), [bT], [bqk])
        for cc in range(2):
            P.op("pe", lambda e, cc=cc: e.matmul(
                psT[0:64, 256:320], lhsT=qk[:, 2 + cc, :], rhs=qk[:, cc, :], start=(cc == 0), stop=(cc == 1)),
                reads=[bqk], writes=[bT], inc=(cc == 1))
        st, bst = sTb[i3]
        P.op("dve", lambda e: e.tensor_tensor(out=st[:], in0=psT[0:64, 256:320], in1=tri[:, :], op=ALU.mult),
             reads=[bT, btri], writes=[bst])
        x_.update(qk=qk, bqk=bqk, st=st, bst=bst)

    def stage3(n):
        gi, c, (qt_, bq), (kt_, bk), (gt_, bgg), (vt_, bv), (ot_, bo) = grp(n)
        x_ = cx[n]
        qk, bqk, st, bst, kbt, bkb, lm, blm = (x_[k_] for k_ in ("qk", "bqk", "st", "bst", "kbt", "bkb", "lm", "blm"))
        psO, bO = kb.bank()
        P.op("pe", lambda e: e.matmul(psO[0:64, 0:256], lhsT=st[:, :], rhs=vt_[:, c, :], start=True, stop=False),
             reads=[bst, bv], writes=[bO], inc=False)
        for cc in range(2):
            P.op("pe", lambda e, cc=cc: e.matmul(
                psO[0:64, 0:256], lhsT=qk[:, cc, :], rhs=Sb[:, cc, :], start=False, stop=(cc == 1)),
                reads=[bqk, bSb], writes=[bO], inc=(cc == 1))
        evac_copy(kb, "act", ot_[:, c, :], psO[0:64, 0:256], [bO], [bo])
        psD, bD = kb.bank()
        for cc in range(2):
            P.op("pe", lambda e, cc=cc: e.matmul(
                psD[:, cc * 256:(cc + 1) * 256], lhsT=kbt[:, cc * 128:(cc + 1) * 128], rhs=vt_[:, c, :],
                start=True, stop=True), reads=[bkb, bv], writes=[bD], inc=(cc == 1))
        tm, btm = tmp[n % 2]
        P.op("dve", lambda e: e.tensor_tensor(
            out=tm[:], in0=psD[:, :], in1=S[:, :, :].rearrange("p a d -> p (a d)"), op=ALU.add),
            reads=[bD, bS], writes=[btm])
        for cc in range(2):
            P.op("dve", lambda e, cc=cc: e.tensor_scalar(
                out=S[:, cc, :], in0=tm[:, cc * 256:(cc + 1) * 256], scalar1=lm[:, 2 * cc:2 * cc + 1], scalar2=None,
                op0=ALU.mult), reads=[btm, blm], writes=[bS])
            P.op("act", lambda e, cc=cc: e.activation(
                out=Sb[:, cc, :], in_=tm[:, cc * 256:(cc + 1) * 256], func=AF.Copy, scale=lm[:, 2 * cc:2 * cc + 1]),
                reads=[btm, blm], writes=[bSb])
        if c == G - 1:
            sl = slice(gi * 64 * G, (gi + 1) * 64 * G)
            P.dma("sp", lambda e: e.dma_start(
                out=o_o[sl, :].rearrange("(c p) d -> p c d", p=64), in_=ot_[:]), reads=[bo], final=True)
        cx[n] = None

    load_group(0)
    for i in range(nch + 2):
        if i < nch:
            stage1(i)
        if 0 <= i - 1 < nch:
            stage2(i - 1)
        if 0 <= i - 2 < nch:
            stage3(i - 2)
    return kb.finish()


def res_proj(kb, ws, aT, baT, T, w_ap, res_src, res_bufs, dst, dst_bufs, stf, rst):
    P = kb.P
    for nb in range(4):
        def evac(tt, ps, bps, nb=nb):
            r, br = rst.next()
            rd = [res_bufs[tt]] if res_bufs is not None else []
            P.dma("sp", lambda e: e.dma_start(out=r[:], in_=res_src[tt * 128:(tt + 1) * 128, nb * 512:(nb + 1) * 512]),
                  reads=rd, writes=[br])
            s_, bs_ = stf.next()
            P.op("dve", lambda e: e.tensor_tensor(out=s_[:], in0=ps[:, :], in1=r[:], op=ALU.add),
                 reads=[bps, br], writes=[bs_])
            P.dma("sp", lambda e: e.dma_start(out=dst[tt * 128:(tt + 1) * 128, nb * 512:(nb + 1) * 512], in_=s_[:]),
                  reads=[bs_], writes=[dst_bufs[tt]])
        proj_tok(kb, ws, aT, baT, T, w_ap, nb * 512, 512, evac)


def emit_mlp(kb, ws, src, src_bufs, gt, gb, w_up, w_dn, dst, dst_bufs, T, xn5, bxn5, big, stf, rst, rl):
    P = kb.P
    TB = min(1024, T)
    nts = TB // 128
    hT = big[:, 0:32 * TB].rearrange("p (f t) -> p f t", f=32)
    bhT = [kb.buf("hT") for _ in range(32)]
    for t0 in range(0, T, TB):
        kb.rows.run(src, t0, nts, gt, gb, src_bufs=src_bufs, dstT=xn5, bdstT=bxn5, col0=0)
        for half in range(2):
            for f4 in range(8):
                def evac(j, tt0, nt, ps, bps, f4=f4):
                    r, br = rl.next()
                    P.op("act", lambda e: e.activation(out=r[:, 0:nt], in_=ps[:, 0:nt], func=AF.Relu), [bps], [br])
                    P.op("dve", lambda e: e.tensor_tensor(out=hT[:, f4 * 4 + j, tt0:tt0 + nt], in0=ps[:, 0:nt],
                                                          in1=r[:, 0:nt], op=ALU.mult), [bps, br], [bhT[f4 * 4 + j]])
                proj_feat(kb, ws, xn5, bxn5, TB, w_up, half * 4096 + f4 * 512, 512, evac, tb=min(512, TB))
            rsrc, rbufs = (src, src_bufs) if half == 0 else (dst, dst_bufs)
            for nb in range(8):
                wt, wb = ws.t[ws.i % len(ws.t)]
                ws.i += 1
                wsrc = w_dn[half * 4096:(half + 1) * 4096, nb * 256:(nb + 1) * 256].rearrange("(kc p) n -> p kc n", p=128)
                wv = wt[:, :, :].rearrange("p k n -> p (k n)")[:, 0:32 * 256].rearrange("p (k n) -> p k n", k=32)
                P.dma("pool", lambda e, wv=wv, wsrc=wsrc: e.dma_start(out=wv, in_=wsrc), writes=[wb])
                for ts in range(nts):
                    ps, bps = kb.bank()
                    for kc in range(32):
                        P.op("pe", lambda e, ps=ps, kc=kc, ts=ts, wv=wv: e.matmul(
                            ps[:, 0:256], lhsT=hT[:, kc, ts * 128:(ts + 1) * 128], rhs=wv[:, kc, :],
                            start=(kc == 0), stop=(kc == 31)),
                            reads=[bhT[kc], wb], writes=[bps], inc=(kc == 31))
                    r0 = t0 + ts * 128
                    tt = r0 // 128
                    r, br = rst.next()
                    rd = [rbufs[tt]] if rbufs is not None else []
                    P.dma("sp", lambda e, r=r, r0=r0, nb=nb, rsrc=rsrc: e.dma_start(
                        out=r[:, 0:256], in_=rsrc[r0:r0 + 128, nb * 256:(nb + 1) * 256]), reads=rd, writes=[br])
                    s_, bs_ = stf.next()
                    P.op("dve", lambda e, s_=s_, ps=ps, r=r: e.tensor_tensor(
                        out=s_[:, 0:256], in0=ps[:, 0:256], in1=r[:, 0:256], op=ALU.add),
                        reads=[bps, br], writes=[bs_])
                    P.dma("sp", lambda e, s_=s_, r0=r0, nb=nb: e.dma_start(
                        out=dst[r0:r0 + 128, nb * 256:(nb + 1) * 256], in_=s_[:, 0:256]), reads=[bs_], writes=[dst_bufs[tt]])


def common_setup(kb, ident_d):
    kb.init_psum()
    load_consts(kb, ident_d)
    kb.eps, kb.beps = kb.sbb("eps", [128, 1], F32)
    kb.P.op("dve", lambda e: e.memset(kb.eps[:], EPS), writes=[kb.beps])
    kb.rows = Rows(kb)


class GVec:
    def __init__(self, kb):
        self.kb = kb
        self.t = [kb.sbb("gvec%d" % i, [128, D], F32) for i in range(2)]
        self.i = 0

    def load(self, vec_ap):
        t, b = self.t[self.i % 2]
        self.i += 1
        self.kb.P.dma("sp", lambda e: e.dma_start(out=t[:], in_=vec_ap.partition_broadcast(128)), writes=[b])
        return t, b


def build_C(T):
    kb = KB()
    P = kb.P
    o = kb.inp("o", [T, D])
    sr = kb.inp("sr", [T, D], BF16)
    x = kb.inp("x", [T, D])
    hg4 = kb.inp("hg4", [D])
    w_out = kb.inp("w_out", [D, D])
    g2 = kb.inp("g2", [D])
    w_up = kb.inp("w_up", [D, DFF])
    w_dn = kb.inp("w_dn", [DFF, D])
    gkv = kb.inp("gkv", [D])
    kv_w = kb.inp("kv_w", [D, 1024])
    g1b = kb.inp("g1b", [D])
    w_q = kb.inp("w_q", [D, D])
    ident_d = kb.inp("ident", [128, 128])
    h1, _ = kb.out("h1", [T, D])
    KT_o, _ = kb.out("KT", [4, 128, T], BF16)
    V_o, _ = kb.out("V", [T, 512], BF16)
    QT_o, _ = kb.out("QT", [16, 128, T], BF16)
    hA = kb.dram("hA", [T, D], F32)
    common_setup(kb, ident_d)
    ntt = T // 128
    bhA = [kb.buf("hA") for _ in range(ntt)]
    bh1 = [kb.buf("h1") for _ in range(ntt)]
    big = kb.sb("big", [128, 32768], BF16)
    big3 = big[:, 0:16 * T].rearrange("p (k t) -> p k t", k=16)
    bbig = [kb.buf("big") for _ in range(ntt)]
    TB = min(1024, T)
    xn5 = kb.sb("xn5", [128, 16, TB], BF16)
    bxn5 = [kb.buf("xn5") for _ in range(TB // 128)]
    ws = WStream(kb)
    gv = GVec(kb)
    stf = Stager(kb, "stf", [128, 512], F32)
    stb = Stager(kb, "stb", [128, 512], BF16)
    rst = Stager(kb, "rst", [128, 512], F32)
    rl = Stager(kb, "rl", [128, 512], F32, n=2)
    gt, gb = gv.load(hg4)
    kb.rows.run(o, 0, ntt, gt, gb, nheads=4, mul_src=sr, dstT=big3, bdstT=bbig)
    res_proj(kb, ws, big3, bbig, T, w_out, x, None, hA, bhA, stf, rst)
    P.barrier()
    gt, gb = gv.load(g2)
    emit_mlp(kb, ws, hA, bhA, gt, gb, w_up, w_dn, h1, bh1, T, xn5, bxn5, big, stf, rst, rl)
    P.barrier()
    gt, gb = gv.load(gkv)
    kb.rows.run(h1, 0, ntt, gt, gb, src_bufs=bh1, dstT=big3, bdstT=bbig)

    def evacK(j, t0, nt, ps, bps):
        s_, bs_ = stb.next()
        evac_copy(kb, kb.evac_eng(), s_[:, 0:nt], ps[:, 0:nt], [bps], [bs_])
        P.dma("sp", lambda e: e.dma_start(out=KT_o[j, :, t0:t0 + nt], in_=s_[:, 0:nt]), reads=[bs_], final=True)
    proj_feat(kb, ws, big3, bbig, T, kv_w, 0, 512, evacK)

    def evacV(tt, ps, bps):
        s_, bs_ = stb.next()
        evac_copy(kb, kb.evac_eng(), s_[:], ps[:, :], [bps], [bs_])
        P.dma("sp", lambda e: e.dma_start(out=V_o[tt * 128:(tt + 1) * 128, :], in_=s_[:]), reads=[bs_], final=True)
    proj_tok(kb, ws, big3, bbig, T, kv_w, 512, 512, evacV)
    gt, gb = gv.load(g1b)
    kb.rows.run(h1, 0, ntt, gt, gb, src_bufs=bh1, dstT=big3, bdstT=bbig)
    for nb in range(4):
        def evacQ(j, t0, nt, ps, bps, nb=nb):
            s_, bs_ = stb.next()
            evac_copy(kb, kb.evac_eng(), s_[:, 0:nt], ps[:, 0:nt], [bps], [bs_], scale=SB_DH ** -0.5)
            P.dma("sp", lambda e: e.dma_start(out=QT_o[nb * 4 + j, :, t0:t0 + nt], in_=s_[:, 0:nt]), reads=[bs_], final=True)
        proj_feat(kb, ws, big3, bbig, T, w_q, nb * 512, 512, evacQ)
    for b in bh1:
        for k_, v_ in b.w.items():
            if P.final.get(k_, 0) < v_:
                P.final[k_] = v_
    return kb.finish()


def build_E(T):
    kb = KB()
    P = kb.P
    OT = kb.inp("OT", [16, 128, T], BF16)
    h1 = kb.inp("h1", [T, D])
    w_out = kb.inp("w_out", [D, D])
    g2 = kb.inp("g2", [D])
    w_up = kb.inp("w_up", [D, DFF])
    w_dn = kb.inp("w_dn", [DFF, D])
    gf = kb.inp("gf", [D])
    ident_d = kb.inp("ident", [128, 128])
    y, _ = kb.out("y", [T, D])
    hB = kb.dram("hB", [T, D], F32)
    h2 = kb.dram("h2", [T, D], F32)
    common_setup(kb, ident_d)
    ntt = T // 128
    bhB = [kb.buf("hB") for _ in range(ntt)]
    bh2 = [kb.buf("h2") for _ in range(ntt)]
    big = kb.sb("big", [128, 32768], BF16)
    big3 = big[:, 0:16 * T].rearrange("p (k t) -> p k t", k=16)
    bbig = [kb.buf("big") for _ in range(ntt)]
    TB = min(1024, T)
    xn5 = kb.sb("xn5", [128, 16, TB], BF16)
    bxn5 = [kb.buf("xn5") for _ in range(TB // 128)]
    ws = WStream(kb)
    gv = GVec(kb)
    stf = Stager(kb, "stf", [128, 512], F32)
    rst = Stager(kb, "rst", [128, 512], F32)
    rl = Stager(kb, "rl", [128, 512], F32, n=2)
    for h in range(16):
        P.dma("sp", lambda e, h=h: e.dma_start(out=big3[:, h, :], in_=OT[h, :, :]), writes=bbig, key=("dma", "OTload"))
    res_proj(kb, ws, big3, bbig, T, w_out, h1, None, hB, bhB, stf, rst)
    P.barrier()
    gt, gb = gv.load(g2)
    emit_mlp(kb, ws, hB, bhB, gt, gb, w_up, w_dn, h2, bh2, T, xn5, bxn5, big, stf, rst, rl)
    gt, gb = gv.load(gf)
    kb.rows.run(h2, 0, ntt, gt, gb, src_bufs=bh2, dst_dram=y, final=True)
    return kb.finish()


def build_D(N):
    kb = KB()
    P = kb.P
    QT2 = kb.inp("QT2", [2, 128, N], BF16)
    KT_d = kb.inp("KT", [128, N], BF16)
    V_d = kb.inp("V", [N, 128], BF16)
    masks_d = kb.inp("masks", [128, 4, 512])
    negU_d = kb.inp("negU", [128, 128], BF16)
    onesb_d = kb.inp("onesb", [128, 128], BF16)
    OT2, _ = kb.out("OT2", [2, 128, N], BF16)
    kb.init_psum()
    obanks = kb.banks[:2]
    kb.banks = kb.banks[2:]
    NB = N // 128
    NQ = N // 512
    KT, bKT = kb.sbb("KT", [128, N], BF16)
    Vt, bVt = kb.sbb("Vt", [128, NB, 128], BF16)
    QT = [kb.sbb("QT%d" % i, [128, N], BF16) for i in range(2)]
    mk, bmk = kb.sbb("mk", [128, 4, 512], F32)
    negU, bnegU = kb.sbb("negU", [128, 128], BF16)
    onesb, bonesb = kb.sbb("onesb", [128, 128], BF16)
    P.dma("sp", lambda e: e.dma_start(out=KT[:], in_=KT_d[:, :]), writes=[bKT])
    P.dma("sp", lambda e: e.dma_start(out=Vt[:], in_=V_d[:, :].rearrange("(b p) d -> p b d", p=128)), writes=[bVt])
    for i in range(2):
        P.dma("sp", lambda e, i=i: e.dma_start(out=QT[i][0][:], in_=QT2[i, :, :]), writes=[QT[i][1]])
    P.dma("sp", lambda e: e.dma_start(out=mk[:], in_=masks_d[:, :, :]), writes=[bmk])
    P.dma("sp", lambda e: e.dma_start(out=negU[:], in_=negU_d[:, :]), writes=[bnegU])
    P.dma("sp", lambda e: e.dma_start(out=onesb[:], in_=onesb_d[:, :]), writes=[bonesb])
    et = Stager(kb, "e", [128, 512], F32, n=3)
    spf = Stager(kb, "spf", [128, 512], F32, n=2)
    spb = Stager(kb, "spb", [128, 512], BF16, n=3)
    lwt = Stager(kb, "lw", [128, 512], F32, n=3)
    wf = Stager(kb, "wf", [128, 512], F32, n=2)
    wb_ = Stager(kb, "wb", [128, 512], BF16, n=3)
    ost = Stager(kb, "ost", [128, 512], BF16, n=2)
    R, bR = kb.sbb("R", [128, 512], F32)
    banksS = kb.banks[0:2]
    banksL = kb.banks[2:4]
    banksT = kb.banks[4:6]
    steps = []
    for h in range(2):
        for qt in range(NQ):
            for kb_ in range(4 * qt + 3, -1, -1):
                steps.append((h, qt, kb_, kb_ == 4 * qt + 3))
    nS = len(steps)
    ctx = [None] * nS
    tile_bank = {}

    def stageA_pe(n):
        h, qt, kb_, first = steps[n]
        Qh, bQh = QT[h]
        qs = slice(qt * 512, (qt + 1) * 512)
        ks = slice(kb_ * 128, (kb_ + 1) * 128)
        psS, bS_ = banksS[n % 2]
        P.op("pe", lambda e: e.matmul(psS[:, :], lhsT=KT[:, ks], rhs=Qh[:, qs], start=True, stop=True),
             reads=[bKT, bQh], writes=[bS_])
        ctx[n] = dict(Qh=Qh, bQh=bQh, qs=qs, ks=ks, d=kb_ - 4 * qt)

    def stageA(n):
        c = ctx[n]
        d = c["d"]
        psS, bS_ = banksS[n % 2]
        e_, be = et.next()
        P.op("act", lambda e: e.activation(out=e_[:], in_=psS[:, :], func=AF.Exp), reads=[bS_], writes=[be])
        sb_, bsb = spb.next()
        if d >= 0:
            sf, bsf = spf.next()
            P.op("act", lambda e: e.activation(out=sf[:], in_=e_[:], func=AF.Ln, bias=1.0), reads=[be], writes=[bsf])
            P.op("dve", lambda e: e.tensor_tensor(out=sb_[:], in0=sf[:], in1=mk[:, d, :], op=ALU.mult),
                 reads=[bsf, bmk], writes=[bsb])
        else:
            P.op("act", lambda e: e.activation(out=sb_[:], in_=e_[:], func=AF.Ln, bias=1.0), reads=[be], writes=[bsb])
        c["sb_"], c["bsb"] = sb_, bsb

    def stageB(n):
        h, qt, kb_, first = steps[n]
        c = ctx[n]
        sb_, bsb, Qh, bQh, qs, ks = c["sb_"], c["bsb"], c["Qh"], c["bQh"], c["qs"], c["ks"]
        psL, bL = banksL[n % 2]
        psT, bT = banksT[n % 2]
        P.op("pe", lambda e: e.matmul(psL[:, :], lhsT=KT[:, ks], rhs=Qh[:, qs], start=True, stop=False),
             reads=[bKT, bQh], writes=[bL], inc=False)
        P.op("pe", lambda e: e.matmul(psL[:, :], lhsT=negU[:, :], rhs=sb_[:], start=False, stop=True),
             reads=[bnegU, bsb], writes=[bL])
        if kb_ > 0:
            P.op("pe", lambda e: e.matmul(psT[:, :], lhsT=onesb[:, :], rhs=sb_[:], start=True, stop=True),
                 reads=[bonesb, bsb], writes=[bT])
        lw, blw = lwt.next()
        if first:
            P.op("dve", lambda e: e.tensor_copy(out=lw[:], in_=psL[:, :]), reads=[bL], writes=[blw])
            P.op("dve", lambda e: e.tensor_copy(out=R[:], in_=psT[:, :]), reads=[bT], writes=[bR])
        else:
            P.op("dve", lambda e: e.tensor_tensor(out=lw[:], in0=psL[:, :], in1=R[:], op=ALU.subtract),
                 reads=[bL, bR], writes=[blw])
            if kb_ > 0:
                P.op("dve", lambda e: e.tensor_tensor(out=R[:], in0=psT[:, :], in1=R[:], op=ALU.add),
                     reads=[bT, bR], writes=[bR])
        c["lw"], c["blw"] = lw, blw

    def stageC(n):
        h, qt, kb_, first = steps[n]
        c = ctx[n]
        lw, blw, d, qs = c["lw"], c["blw"], c["d"], c["qs"]
        if first:
            tile_bank[(h, qt)] = obanks[len(tile_bank) % 2]
        psO, bO = tile_bank[(h, qt)]
        w_, bw = wb_.next()
        if d >= 0:
            wf_, bwf = wf.next()
            P.op("act", lambda e: e.activation(out=wf_[:], in_=lw[:], func=AF.Exp), reads=[blw], writes=[bwf])
            P.op("dve", lambda e: e.tensor_tensor(out=w_[:], in0=wf_[:], in1=mk[:, d, :], op=ALU.mult),
                 reads=[bwf, bmk], writes=[bw])
        else:
            P.op("act", lambda e: e.activation(out=w_[:], in_=lw[:], func=AF.Exp), reads=[blw], writes=[bw])
        P.op("pe", lambda e: e.matmul(psO[:, :], lhsT=Vt[:, kb_, :], rhs=w_[:], start=first, stop=(kb_ == 0)),
             reads=[bVt, bw], writes=[bO])
        if kb_ == 0:
            os_, bos = ost.next()
            evac_copy(kb, "dve", os_[:], psO[:, :], [bO], [bos])
            P.dma("sp", lambda e: e.dma_start(out=OT2[h, :, qs], in_=os_[:]), reads=[bos], final=True)
        ctx[n] = None

    stageA_pe(0)
    for i in range(nS + 2):
        if i + 1 < nS:
            stageA_pe(i + 1)
        if i < nS:
            stageA(i)
        if 0 <= i - 1 < nS:
            stageB(i - 1)
        if 0 <= i - 2 < nS:
            stageC(i - 2)
    return kb.finish()


_CACHE = {}


def _get(name, fn, *a):
    key = (name,) + a
    if key not in _CACHE:
        _CACHE[key] = fn(*a)
    return _CACHE[key]


def _run(nc, in_maps):
    res = run_bass_kernel_spmd(nc, in_maps, core_ids=list(range(NCORES)))
    return res.results


def attn_consts():
    import ml_dtypes
    j = np.arange(128)[:, None, None]
    d = np.arange(4)[None, :, None]
    i = np.arange(512)[None, None, :]
    masks = (j + 128 * d < i).astype(np.float32)
    negU = (-(np.arange(128)[:, None] >= np.arange(128)[None, :]).astype(np.float32)).astype(ml_dtypes.bfloat16)
    onesb = np.ones((128, 128), np.float32).astype(ml_dtypes.bfloat16)
    return masks, negU, onesb


def kernel(x, norm1_g, norm2_g, gla_w_in, gla_w_gate_up, gla_b_gate, gla_head_g, gla_w_out, kv_norm_g, kv_w,
           sb_w_q, sb_w_out, mlp_w_up, mlp_w_down, final_g):
    import ml_dtypes
    f = lambda a: np.ascontiguousarray(np.asarray(a, dtype=np.float32))
    x2 = f(x)[0]
    S = x2.shape[0]
    T = S // NCORES
    ident = np.eye(128, dtype=np.float32)
    cs = [slice(c * T, (c + 1) * T) for c in range(NCORES)]
    w_in = f(gla_w_in[0]); wgu = f(gla_w_gate_up[0]); bg = f(gla_b_gate[0]); g1 = f(norm1_g[0])
    rA = _run(_get("A", build_A, T), [{"x": f(x2[cs[c]]), "g1": g1, "w_in": w_in, "wgu": wgu, "bg": bg, "ident": ident}
                                      for c in range(NCORES)])
    cat = lambda n: np.concatenate([np.asarray(rA[c][n]) for c in range(NCORES)], 0)
    qf, kf, gf_, vf, srf = cat("q"), cat("k"), cat("g"), cat("v"), cat("sr")
    tri = np.triu(np.ones((64, 64), np.float32))
    identb = np.eye(64, dtype=np.float32).astype(ml_dtypes.bfloat16)
    ones = np.ones((64, 2), np.float32)
    mapsB = []
    for c in range(NCORES):
        h, d = c // 2, c % 2
        hs = slice(h * 256, (h + 1) * 256)
        mapsB.append({"q": f(qf[:, hs]), "k": f(kf[:, hs]), "g": f(gf_[:, hs]),
                      "v": np.ascontiguousarray(vf[:, h * 512 + d * 256: h * 512 + (d + 1) * 256]),
                      "tri": tri, "identb": identb, "ones": ones})
    rB = _run(_get("B", build_B, S), mapsB)
    of = np.empty((S, D), np.float32)
    for c in range(NCORES):
        h, d = c // 2, c % 2
        of[:, h * 512 + d * 256: h * 512 + (d + 1) * 256] = np.asarray(rB[c]["o"])
    hg4 = np.ascontiguousarray(np.tile(f(gla_head_g[0]), 4))
    cst = {"hg4": hg4, "w_out": f(gla_w_out[0]), "g2": f(norm2_g[0]), "w_up": f(mlp_w_up[0]), "w_dn": f(mlp_w_down[0]),
           "gkv": f(kv_norm_g), "kv_w": f(kv_w), "g1b": f(norm1_g[1]), "w_q": f(sb_w_q[0]), "ident": ident}
    rC = _run(_get("C", build_C, T), [dict(cst, o=f(of[cs[c]]), sr=np.ascontiguousarray(srf[cs[c]]), x=f(x2[cs[c]]))
                                      for c in range(NCORES)])
    h1 = [np.asarray(rC[c]["h1"]) for c in range(NCORES)]
    KTf = np.concatenate([np.asarray(rC[c]["KT"]) for c in range(NCORES)], 2)
    Vf = np.concatenate([np.asarray(rC[c]["V"]) for c in range(NCORES)], 0)
    QTf = np.concatenate([np.asarray(rC[c]["QT"]) for c in range(NCORES)], 2)
    masks, negU, onesb = attn_consts()
    mapsD = []
    for c in range(NCORES):
        kvh = c // 2
        mapsD.append({"QT2": np.ascontiguousarray(QTf[2 * c:2 * c + 2]), "KT": np.ascontiguousarray(KTf[kvh]),
                      "V": np.ascontiguousarray(Vf[:, kvh * 128:(kvh + 1) * 128]),
                      "masks": masks, "negU": negU, "onesb": onesb})
    rD = _run(_get("D", build_D, S), mapsD)
    OTf = np.concatenate([np.asarray(rD[c]["OT2"]) for c in range(NCORES)], 0)
    cst = {"w_out": f(sb_w_out[0]), "g2": f(norm2_g[1]), "w_up": f(mlp_w_up[1]), "w_dn": f(mlp_w_down[1]),
           "gf": f(final_g), "ident": ident}
    rE = _run(_get("E", build_E, T), [dict(cst, OT=np.ascontiguousarray(OTf[:, :, cs[c]]), h1=h1[c]) for c in range(NCORES)])
    y = np.concatenate([np.asarray(rE[c]["y"]) for c in range(NCORES)], 0)
    return y.reshape(1, S, D).astype(np.float32)
```
